# Optimizing a Trainium2 kernel written in Bass

```python
import jax, jax.numpy as jnp
from jax import lax
import numpy as np

D_MODEL = 1024
BATCH = 8
SEQ = 2048
DEPTH = 2
DEC_BATCH = 128
DEC_SEQ = 1
PAST_LEN = 16384
PAGE_SIZE = 128

N_META = 16
N_MIXERS = 2
N_POOL_LAYERS = (DEPTH + 1) // 2
N_RET_LAYERS = DEPTH // 2
POOL_WINDOWS = (2, 4, 8, 16)
POOL_GROUPS = len(POOL_WINDOWS)
POOL_GROUP_DIM = D_MODEL // POOL_GROUPS
POOL_BUF = max(POOL_WINDOWS) - 1
RET_HEADS = 4
RET_KDIM = D_MODEL // RET_HEADS
RET_VDIM = 2 * D_MODEL // RET_HEADS
RET_CHUNK = 128
ROPE_BASE = 10000.0
D_FF_RAW = -(-8 * D_MODEL // 3)
D_FF = -(-D_FF_RAW // 256) * 256
EPS = 1e-6

kernel_name = 'pool_retention_hybrid_step'


def rmsnorm(x, g):
    xf = x.astype(jnp.float32)
    y = xf * lax.rsqrt(jnp.mean(xf * xf, axis=-1, keepdims=True) + EPS)
    return (y * g.astype(jnp.float32)).astype(x.dtype)


def swiglu(x, w_gate, w_up, w_down):
    return (jax.nn.silu(x @ w_gate) * (x @ w_up)) @ w_down


def multiscale_pool_mix(xn, prev, w_pool, scale):
    b, t, _ = xn.shape
    p = prev.shape[1]
    ext = jnp.concatenate([prev.astype(xn.dtype), xn], axis=1).astype(jnp.float32)
    cs = jnp.concatenate([jnp.zeros((b, 1, D_MODEL), jnp.float32), jnp.cumsum(ext, axis=1)], axis=1)
    end = p + jnp.arange(t) + 1
    groups = []
    for g, w in enumerate(POOL_WINDOWS):
        start = jnp.maximum(end - w, 0)
        sl = slice(g * POOL_GROUP_DIM, (g + 1) * POOL_GROUP_DIM)
        s = cs[:, end, sl] - cs[:, start, sl]
        cnt = (end - start).astype(jnp.float32)
        groups.append(s / cnt[None, :, None])
    pooled = jnp.stack(groups, axis=2)
    diff = (pooled - xn.astype(jnp.float32).reshape(b, t, POOL_GROUPS, POOL_GROUP_DIM)).astype(xn.dtype)
    mixed = jnp.einsum('btgc,gcd->btgd', diff, w_pool).reshape(b, t, D_MODEL)
    return mixed * scale, ext[:, -POOL_BUF:].astype(xn.dtype)


def rotary(x, pos):
    theta = 1.0 / (ROPE_BASE ** jnp.linspace(0.0, 1.0, RET_KDIM // 2, dtype=jnp.float32))
    ang = pos.astype(jnp.float32)[:, None] * theta[None, :]
    cos = jnp.cos(ang)[None, :, None, :]
    sin = jnp.sin(ang)[None, :, None, :]
    xf = x.astype(jnp.float32).reshape(*x.shape[:-1], RET_KDIM // 2, 2)
    x1, x2 = xf[..., 0], xf[..., 1]
    out = jnp.stack([x1 * cos - x2 * sin, x1 * sin + x2 * cos], axis=-1).reshape(x.shape)
    return out.astype(x.dtype)


def ret_chunk(q, k, v, s0, log_g):
    c = q.shape[1]
    idx = jnp.arange(c, dtype=jnp.float32)
    rel = idx[:, None] - idx[None, :]
    dmask = jnp.exp(jnp.where(rel[None] >= 0, rel[None] * log_g[:, None, None], -jnp.inf)).astype(q.dtype)
    scores = jnp.einsum('bihd,bjhd->bhij', q, k) * dmask
    o = jnp.einsum('bhij,bjhe->bihe', scores, v)
    cross = jnp.exp((idx[:, None] + 1.0) * log_g[None, :]).astype(q.dtype)
    o = o + jnp.einsum('bihd,bhde->bihe', q, s0.astype(q.dtype)) * cross[None, :, :, None]
    kdec = jnp.exp((c - 1.0 - idx)[:, None] * log_g[None, :]).astype(k.dtype)
    s_new = (jnp.exp(c * log_g)[None, :, None, None].astype(s0.dtype) * s0
             + jnp.einsum('bjhd,bjhe->bhde', k * kdec[None, :, :, None], v).astype(s0.dtype))
    return o, s_new


def retention_mix(xn, pos, s0, log_g, w_in, gn_w, gn_b, w_out, chunked):
    b, t, _ = xn.shape
    proj = xn @ w_in
    q, k, v, g = jnp.split(proj, [D_MODEL, 2 * D_MODEL, 4 * D_MODEL], axis=-1)
    q = rotary(q.reshape(b, t, RET_HEADS, RET_KDIM), pos) * (RET_KDIM ** -0.5)
    k = rotary(k.reshape(b, t, RET_HEADS, RET_KDIM), pos)
    v = v.reshape(b, t, RET_HEADS, RET_VDIM)
    if chunked:
        o0, s = ret_chunk(q[:, :N_META], k[:, :N_META], v[:, :N_META], s0, log_g)
        t_real = t - N_META
        nc = t_real // RET_CHUNK

        def to_chunks(a):
            return a[:, N_META:].reshape(b, nc, RET_CHUNK, RET_HEADS, a.shape[-1]).swapaxes(0, 1)

        def step(state, qkv):
            oc, state = ret_chunk(qkv[0], qkv[1], qkv[2], state, log_g)
            return state, oc

        s, oc = lax.scan(step, s, (to_chunks(q), to_chunks(k), to_chunks(v)))
        o = jnp.concatenate([o0, oc.swapaxes(0, 1).reshape(b, t_real, RET_HEADS, RET_VDIM)], axis=1)
    else:
        o, s = ret_chunk(q, k, v, s0, log_g)
    of = o.astype(jnp.float32)
    mu = jnp.mean(of, axis=-1, keepdims=True)
    var = jnp.var(of, axis=-1, keepdims=True)
    on = ((of - mu) * lax.rsqrt(var + EPS)).reshape(b, t, RET_HEADS * RET_VDIM)
    on = (on * gn_w.astype(jnp.float32) + gn_b.astype(jnp.float32)).astype(xn.dtype)
    return (jax.nn.silu(g) * on) @ w_out, s


def setup_inputs(seed: int = 0) -> dict:
    key = jax.random.key(seed)
    ks = jax.random.split(key, 18)
    f32 = jnp.float32
    nrm = lambda k, shape, s: jax.random.normal(k, shape, f32) * s
    return {
        'x_prompt': nrm(ks[0], (BATCH, SEQ, D_MODEL), 1.0),
        'x_sample': nrm(ks[1], (DEC_BATCH, DEC_SEQ, D_MODEL), 1.0),
        'state_pool': nrm(ks[2], (N_POOL_LAYERS, DEC_BATCH, POOL_BUF, D_MODEL), 1.0),
        'state_ret': nrm(ks[3], (N_RET_LAYERS, DEC_BATCH, RET_HEADS, RET_KDIM, RET_VDIM), 0.1),
        'meta_tokens': nrm(ks[4], (N_META, D_MODEL), 1.0),
        'norm_mix': 1.0 + nrm(ks[5], (DEPTH, D_MODEL), 0.02),
        'norm_ffn': 1.0 + nrm(ks[6], (DEPTH, D_MODEL), 0.02),
        'norm_final': 1.0 + nrm(ks[7], (D_MODEL,), 0.02),
        'w_pool': nrm(ks[8], (N_POOL_LAYERS, POOL_GROUPS, POOL_GROUP_DIM, POOL_GROUP_DIM), POOL_GROUP_DIM ** -0.5),
        'pool_scale': 1.0 + nrm(ks[9], (N_POOL_LAYERS, D_MODEL), 0.02),
        'w_ret_in': nrm(ks[10], (N_RET_LAYERS, D_MODEL, 6 * D_MODEL), D_MODEL ** -0.5),
        'ret_gn_w': 1.0 + nrm(ks[11], (N_RET_LAYERS, 2 * D_MODEL), 0.02),
        'ret_gn_b': nrm(ks[12], (N_RET_LAYERS, 2 * D_MODEL), 0.02),
        'w_ret_out': nrm(ks[13], (N_RET_LAYERS, 2 * D_MODEL, D_MODEL), (2 * D_MODEL) ** -0.5),
        'w_ffn_gate': nrm(ks[14], (DEPTH, D_MODEL, D_FF), D_MODEL ** -0.5),
        'w_ffn_up': nrm(ks[15], (DEPTH, D_MODEL, D_FF), D_MODEL ** -0.5),
        'w_ffn_down': nrm(ks[16], (DEPTH, D_FF, D_MODEL), D_FF ** -0.5),
    }


def reference(x_prompt, x_sample, state_pool, state_ret, meta_tokens, norm_mix, norm_ffn, norm_final,
              w_pool, pool_scale, w_ret_in, ret_gn_w, ret_gn_b, w_ret_out, w_ffn_gate, w_ffn_up, w_ffn_down):
    log_gamma = jnp.log(1.0 - 2.0 ** (-5.0 - jnp.arange(RET_HEADS, dtype=jnp.float32)))

    def trunk(h, pos, pool_prev, ret_prev, chunked):
        new_pool, new_ret = [], []
        for i in range(DEPTH):
            j = i // N_MIXERS
            xn = rmsnorm(h, norm_mix[i])
            if i % N_MIXERS == 0:
                mix, buf = multiscale_pool_mix(xn, pool_prev[j], w_pool[j], pool_scale[j])
                new_pool.append(buf)
            else:
                mix, s = retention_mix(xn, pos, ret_prev[j], log_gamma, w_ret_in[j], ret_gn_w[j],
                                       ret_gn_b[j], w_ret_out[j], chunked)
                new_ret.append(s)
            h = h + mix
            h = h + swiglu(rmsnorm(h, norm_ffn[i]), w_ffn_gate[i], w_ffn_up[i], w_ffn_down[i])
        return rmsnorm(h, norm_final), jnp.stack(new_pool), jnp.stack(new_ret)

    b = x_prompt.shape[0]
    meta = jnp.broadcast_to(meta_tokens[None].astype(x_prompt.dtype), (b, N_META, D_MODEL))
    h_p = jnp.concatenate([meta, x_prompt], axis=1)
    pos_p = jnp.arange(N_META + x_prompt.shape[1], dtype=jnp.int32)
    pool_prev_p = jnp.zeros((N_POOL_LAYERS, b, 0, D_MODEL), x_prompt.dtype)
    ret_prev_p = jnp.zeros((N_RET_LAYERS, b, RET_HEADS, RET_KDIM, RET_VDIM), state_ret.dtype)
    out_p, new_pool_prompt, new_ret_prompt = trunk(h_p, pos_p, pool_prev_p, ret_prev_p, True)
    y_prompt = out_p[:, N_META:]

    pos_s = PAST_LEN + jnp.arange(x_sample.shape[1], dtype=jnp.int32)
    y_sample, new_pool_sample, new_ret_sample = trunk(x_sample, pos_s, state_pool, state_ret, False)
    return (y_prompt, y_sample, new_pool_prompt, new_pool_sample, new_ret_prompt, new_ret_sample)
```

```python
import contextlib
import numpy as np
import concourse.bass as bass
import concourse.mybir as mybir
from concourse.bass_utils import run_bass_kernel_spmd

F32 = mybir.dt.float32
BF16 = mybir.dt.bfloat16
ACT = mybir.ActivationFunctionType
ALU = mybir.AluOpType

ENGS = ('pe', 'act', 'dve', 'pool', 'sp')
NCORES = 8
NT = 1040
TBS = ((0, 347), (347, 347), (694, 346))
EPS = 1e-6
WINS = (2, 4, 8, 16)

C_NM0, C_NM1, C_NF0, C_NF1, C_NFIN, C_PS, C_GNW, C_GNB = 0, 8, 16, 24, 32, 40, 48, 64
C_KDEC, C_KDECM, C_MASK, C_CROSS, C_INVC, C_I16, C_D16, C_AW, C_ID = 80, 84, 88, 600, 1112, 1176, 1192, 1448, 1576
NCF = 1704


class Op:
    __slots__ = ('eng', 'fn', 'deps', 'signal', 'sem', 'sigval', 'is_dma')

    def __init__(self, eng, fn):
        self.eng = eng
        self.fn = fn
        self.deps = ()
        self.signal = False
        self.sem = None
        self.sigval = 0
        self.is_dma = False


class Prog:
    def __init__(self, nc, stack):
        self.nc = nc
        self.stack = stack
        self.ops = {e: [] for e in ENGS}
        self.sems = {e: stack.enter_context(nc.semaphore("s_" + e)) for e in ENGS if e != 'sp'}
        self.last_w = {}
        self.readers = {}
        self.dma_sems = {}
        self.out_dmas = []

    def _mk(self, eng, fn, reads, writes):
        o = Op(eng, fn)
        deps = set()
        for k in reads:
            w = self.last_w.get(k)
            if w is not None:
                deps.add(w)
        for k in writes:
            w = self.last_w.get(k)
            if w is not None:
                deps.add(w)
            deps.update(self.readers.get(k, ()))
        o.deps = deps
        for d in deps:
            d.signal = True
        for k in writes:
            self.last_w[k] = o
            self.readers[k] = []
        for k in reads:
            self.readers.setdefault(k, []).append(o)
        self.ops[eng].append(o)
        return o

    def op(self, eng, fn, reads=(), writes=()):
        return self._mk(eng, fn, reads, writes)

    def dma(self, eng, pairs, reads=(), writes=(), semkey=None, is_out=False):
        if semkey is None:
            semkey = (tuple(writes) + tuple(reads))[0]
        ent = self.dma_sems.get(semkey)
        if ent is None:
            sem = self.stack.enter_context(self.nc.semaphore("d%d" % len(self.dma_sems)))
            ent = [sem, 0]
            self.dma_sems[semkey] = ent
        sem = ent[0]

        def fn(h, pairs=pairs, sem=sem):
            for (o_ap, i_ap) in pairs:
                h.dma_start(out=o_ap, in_=i_ap).then_inc(sem, 16)
            return None

        o = self._mk(eng, fn, reads, writes)
        o.is_dma = True
        ent[1] += 16 * len(pairs)
        o.sem = sem
        o.sigval = ent[1]
        if is_out:
            self.out_dmas.append(o)
        return o

    def finish(self):
        o = Op('sp', lambda h: None)
        o.deps = set(self.out_dmas)
        self.ops['sp'].append(o)

    def emit(self):
        nc = self.nc
        for e in ENGS:
            c = 0
            for o in self.ops[e]:
                if not o.is_dma and o.signal:
                    c += 1
                    o.sigval = c
                    o.sem = self.sems[e]
        stats = {}

        def run(e, h):
            seen = {}
            nw = 0
            for o in self.ops[e]:
                waits = {}
                for d in o.deps:
                    if (not d.is_dma) and d.eng == 'pe' and e == 'pe':
                        continue
                    if waits.get(d.sem, (0, 0))[0] < d.sigval:
                        waits[d.sem] = (d.sigval, d.sem)
                for v, sem in waits.values():
                    if seen.get(sem, 0) < v:
                        h.wait_ge(sem, v)
                        seen[sem] = v
                        nw += 1
                ins = o.fn(h)
                if (not o.is_dma) and o.signal:
                    ins.then_inc(o.sem, 1)
            stats[e] = (len(self.ops[e]), nw)

        with nc.Block() as block:
            @block.tensor
            def _(h):
                run('pe', h)

            @block.scalar
            def _(h):
                run('act', h)

            @block.vector
            def _(h):
                run('dve', h)

            @block.gpsimd
            def _(h):
                run('pool', h)

            @block.sync
            def _(h):
                run('sp', h)
        return stats


class Ring:
    def __init__(self, name, tiles):
        self.name = name
        self.tiles = tiles
        self.i = 0

    def next(self):
        i = self.i % len(self.tiles)
        self.i += 1
        return self.tiles[i], (self.name, i)


def _gammas():
    lg = np.log(np.float32(1.0) - np.float32(2.0) ** (-5.0 - np.arange(4, dtype=np.float32))).astype(np.float32)
    return lg


def build_program(dbg=False):
    nc = bass.Bass("TRN2", target_bir_lowering=False)

    def DI(name, shape):
        return nc.dram_tensor(name, list(shape), F32, kind="ExternalInput").ap()

    def DO(name, shape):
        return nc.dram_tensor(name, list(shape), F32, kind="ExternalOutput").ap()

    xT = DI("xT", [1024, 2 * NT])
    csd = DI("cs", [2, 128, 2 * NT])
    cfd = DI("cf", [128, NCF])
    sp_prev = DI("sp_prev", [240, 1024])
    sret = DI("sret", [16, 4, 2, 128, 512])
    wpool_d = DI("wpool", [128, 2048])
    w_gu = DI("w_gu", [2, 11, 128, 4096])
    w_d0 = DI("w_d0", [2, 4, 128, 3072])
    w_d1 = DI("w_d1", [2, 4, 128, 2560])
    w_in = DI("w_in", [4, 3, 128, 4096])
    w_out = DI("w_out", [4, 128, 4096])
    yT = DO("yT", [1024, 2 * NT])
    npp = DO("npp", [1024, 16])
    nps_x = DO("nps_x", [1024, 16])
    nps_prev = DO("nps_prev", [16, 14, 1024])
    nrp = DO("nrp", [4, 2, 128, 512])
    nrs = DO("nrs", [16, 4, 2, 128, 512])
    sscr = nc.dram_tensor("sscr", [4, 2, 128, 512], F32, kind="Internal").ap()
    if dbg:
        dbg_h = DO("dbg_h", [4, 2, 1024, NT])

    lg = _gammas()
    gam = [float(np.exp(lg[h])) for h in range(4)]
    gam128 = [float(np.exp(np.float32(128.0) * lg[h])) for h in range(4)]

    st = contextlib.ExitStack()
    with st:
        P = Prog(nc, st)

        def SB(name, shape, dt):
            return st.enter_context(nc.sbuf_tensor(name, list(shape), dt))

        def PS(name, shape, dt):
            return st.enter_context(nc.psum_tensor(name, list(shape), dt))

        hT = SB("hT", [128, 8, NT], F32)
        xn = SB("xn", [128, 8, NT], BF16)
        RA = SB("RA", [128, 2112], F32)
        RB = SB("RB", [128, 3264], F32)
        RC = SB("RC", [128, 2112], F32)
        cs = SB("cs_sb", [128, 2, NT], F32)
        cf = SB("cf_sb", [128, NCF], F32)
        w32 = SB("w32", [128, 40], F32)
        identb = SB("identb", [128, 128], BF16)
        onesb = SB("onesb", [128, 128], BF16)
        epsb = SB("epsb", [128, 2], F32)
        rstd = SB("rstd", [128, NT], F32)
        S32w = SB("S32w", [128, 2, 512], F32)
        SbfR = [SB("Sbf%d" % i, [128, 2, 512], BF16) for i in range(3)]
        wslots = [SB("wslot%d" % i, [128, 4096], BF16) for i in range(4)]
        sring = Ring('sr', [SB("sring%d" % i, [128, 2, 512], F32) for i in range(4)])
        rotS = [[SB("rot%d_%d" % (j, i), [128, 352], F32) for i in range(4)] for j in range(2)]
        sgR = Ring('sg', [SB("sg%d" % i, [128, 512], BF16) for i in range(2)])
        pmA = SB("pmA", [128, 9, 128], BF16)
        qcA = SB("qcA", [128, 9, 2, 128], BF16)
        kdA = SB("kdA", [128, 9, 256], BF16)
        kdS = SB("kdS", [128, 256], BF16)
        idw = SB("idw", [128, 8, 128], BF16)
        P2 = SB("P2", [128, 2, 32], F32)
        Q2 = SB("Q2", [128, 2, 32], F32)
        vS = SB("vS", [128, 512], BF16)
        oacc = SB("oacc", [128, 512], F32)
        RC2 = SB("RC2", [128, 2080], F32)
        onR = Ring('on', [SB("on%d" % i, [128, 512], BF16) for i in range(3)])
        affR = Ring('aff', [SB("aff%d" % i, [128, 4, 128], F32) for i in range(1)])
        stR = Ring('st', [SB("stt%d" % i, [128, 12], F32) for i in range(2)])
        halo = SB("halo", [128, 8, 16], F32)
        xs = SB("xs", [128, 8, 16], F32)
        prevsum = SB("prevsum", [128, 8, 16], F32)
        kmaskR = Ring('kmask', [SB("kmask%d" % i, [128, 4, 256], BF16) for i in range(1)])
        snbR = [SB("snb%d" % i, [128, 2, 512], BF16) for i in range(2)]
        qm = SB("qm", [128, 2, 16, 16], BF16)
        qsf = SB("qsf", [128, 2, 16], F32)

        pbR = Ring('pb', [PS("pb%d" % i, [128, 512], F32) for i in range(6)])
        psU = PS("psU", [128, 2, 512], F32)

        RA_bf = RA[:, 0:2080].bitcast(BF16).rearrange("p (k n) -> p k n", k=4)
        RA_x = RA[:, 0:2112].rearrange("p (k n) -> p k n", k=2)
        RB_hid = RB[:, 0:3120].bitcast(BF16).rearrange("p (k n) -> p k n", k=6)
        RB_v = RB[:, 0:2304].bitcast(BF16).rearrange("p (k n) -> p k n", k=9)
        RB_diff = RB[:, 0:1040].bitcast(BF16).rearrange("p (k n) -> p k n", k=2)
        RB_q = RB[:, 1040:3152].rearrange("p (k n) -> p k n", k=2)
        RC_bf = RC[:, 0:2080].bitcast(BF16).rearrange("p (k n) -> p k n", k=4)
        RC_p = RC[:, 0:2112].rearrange("p (k n) -> p k n", k=2)
        RC_prev = RC[:, 0:2048].rearrange("p (k n) -> p k n", k=2)
        RC2_bf = RC2[:, 0:2080].bitcast(BF16).rearrange("p (k n) -> p k n", k=4)
        GB = [(RC_bf, 'RC'), (RC2_bf, 'RC2')]

        def hid(i):
            if i < 4:
                return RA_bf[:, i, :], 'RA'
            if i < 10:
                return RB_hid[:, i - 4, :], 'RB'
            return RC_bf[:, i - 10, :], 'RC'

        def hk(c):
            return [('h', c, t) for t in range(3)]

        wl = []
        for half in range(2):
            wl.append(('pool', wpool_d, lambda s: s[:, 0:2048]))
            for l in range(2):
                if l == 1:
                    def w_in_t(h, j):
                        return ('in', w_in[h, j].rearrange("p (k n) -> p k n", k=8),
                                lambda s: s[:, 0:4096].rearrange("p (k n) -> p k n", k=8))

                    def w_out_t(h):
                        return ('out', w_out[h].rearrange("p (k n) -> p k n", k=4),
                                lambda s: s[:, 0:4096].rearrange("p (k n) -> p k n", k=4))
                    for h in range(4):
                        for j in range(3):
                            wl.append(w_in_t(h, j))
                        if half == 0:
                            wl.append(w_out_t(h))
                        elif h > 0:
                            wl.append(w_out_t(h - 1))
                    if half == 1:
                        wl.append(w_out_t(3))
                for fh in range(2):
                    ntile = 6 if fh == 0 else 5
                    for j in range(ntile):
                        jj = j if fh == 0 else 6 + j
                        wl.append(('gu', w_gu[l, jj].rearrange("p (k n) -> p k n", k=8),
                                   lambda s: s[:, 0:4096].rearrange("p (k n) -> p k n", k=8)))
                    nk = 12 if fh == 0 else 10
                    wd = w_d0 if fh == 0 else w_d1
                    for mt in range(4):
                        wl.append(('d', wd[l, mt].rearrange("p (k n) -> p k n", k=nk),
                                   lambda s, nk=nk: s[:, 0:nk * 256].rearrange("p (k n) -> p k n", k=nk)))
        wstate = {'issued': 0, 'cur': 0}
        PREF = 3

        def wget(kind):
            i = wstate['cur']
            assert wl[i][0] == kind, (i, wl[i][0], kind)
            while wstate['issued'] < min(len(wl), i + PREF + 1):
                j = wstate['issued']
                s = j % 4
                view = wl[j][2](wslots[s])
                P.dma('pool', [(view, wl[j][1])], writes=[('w', s)])
                wstate['issued'] += 1
            wstate['cur'] += 1
            s = i % 4
            return wl[i][2](wslots[s]), ('w', s)

        P.dma('sp', [(cf[:], cfd[:, :])], writes=['cf'])
        P.op('act', lambda h: h.mul(out=w32[:], in_=cf[:, 0:40], mul=32.0), reads=['cf'], writes=['w32'])
        P.op('dve', lambda h: h.tensor_copy(out=identb[:], in_=cf[:, C_ID:C_ID + 128]), reads=['cf'], writes=['identb'])
        P.op('pool', lambda h: h.memset(onesb[:], 1.0), writes=['onesb'])
        for g_, w_ in enumerate(WINS):
            P.op('act', lambda h, g_=g_, w_=w_: h.mul(out=idw[:, 2 * g_, :], in_=identb[:], mul=1.0 / w_ - 1.0),
                 reads=['identb'], writes=[('idw', 2 * g_)])
            P.op('act', lambda h, g_=g_, w_=w_: h.mul(out=idw[:, 2 * g_ + 1, :], in_=identb[:], mul=1.0 / w_),
                 reads=['identb'], writes=[('idw', 2 * g_ + 1)])
        P.op('pool', lambda h: h.memset(epsb[:, 0:1], 1024.0 * EPS), writes=['epsb'])
        P.op('pool', lambda h: h.memset(epsb[:, 1:2], EPS), writes=['epsb'])

        def mm(out, lhsT, rhs, start, stop, reads, writes):
            return P.op('pe', lambda h: h.matmul(out, lhsT=lhsT, rhs=rhs, start=start, stop=stop), reads=reads, writes=writes)

        def tr(out, in_, ident, reads, writes):
            return P.op('pe', lambda h: h.transpose(out, in_, ident), reads=reads, writes=writes)

        def norm(wcol, kind):
            for c in range(8):
                if c in (2, 5):
                    P.op('pool', lambda h, c=c: h.tensor_tensor(out=xn[:, c, :], in0=hT[:, c, :], in1=hT[:, c, :], op=ALU.mult),
                         reads=hk(c), writes=[('xn', c)])
                else:
                    P.op('act', lambda h, c=c: h.activation(out=xn[:, c, :], in_=hT[:, c, :], func=ACT.Square),
                         reads=hk(c), writes=[('xn', c)])
            for ti, (t0, tn) in enumerate(TBS):
                pb, pk = pbR.next()
                for c in range(8):
                    mm(pb[:, 0:tn], onesb[:], xn[:, c, t0:t0 + tn], c == 0, c == 7,
                       ['onesb', ('xn', c)], [pk])
                P.op('act', lambda h, pb=pb, t0=t0, tn=tn: h.activation(
                    out=rstd[:, t0:t0 + tn], in_=pb[:, 0:tn], func=ACT.Sqrt, bias=epsb[:, 0:1], scale=1.0),
                    reads=[pk, 'epsb'], writes=[('rstd', ti)])
                P.op('dve', lambda h, t0=t0, tn=tn: h.reciprocal(out=rstd[:, t0:t0 + tn], in_=rstd[:, t0:t0 + tn]),
                     reads=[('rstd', ti)], writes=[('rstd', ti)])
            rk = [('rstd', t) for t in range(3)]
            if kind == 'bf16':
                for c in range(8):
                    P.op('dve', lambda h, c=c: h.scalar_tensor_tensor(
                        out=xn[:, c, :], in0=hT[:, c, :], scalar=w32[:, wcol + c:wcol + c + 1], in1=rstd[:],
                        op0=ALU.mult, op1=ALU.mult), reads=hk(c) + rk + ['w32'], writes=[('xn', c)])
            elif kind == 'final':
                for c in range(8):
                    P.op('dve', lambda h, c=c: h.scalar_tensor_tensor(
                        out=hT[:, c, :], in0=hT[:, c, :], scalar=w32[:, wcol + c:wcol + c + 1], in1=rstd[:],
                        op0=ALU.mult, op1=ALU.mult), reads=hk(c) + rk + ['w32'], writes=hk(c))

        def hadd(m, ti, t0, tn, pb, pk, scale_ap=None):
            if scale_ap is None:
                P.op('dve', lambda h: h.tensor_tensor(out=hT[:, m, t0:t0 + tn], in0=pb[:, 0:tn],
                                                       in1=hT[:, m, t0:t0 + tn], op=ALU.add),
                     reads=[pk, ('h', m, ti)], writes=[('h', m, ti)])
            else:
                P.op('dve', lambda h: h.scalar_tensor_tensor(out=hT[:, m, t0:t0 + tn], in0=pb[:, 0:tn],
                                                              scalar=scale_ap, in1=hT[:, m, t0:t0 + tn],
                                                              op0=ALU.mult, op1=ALU.add),
                     reads=[pk, ('h', m, ti), 'cf'], writes=[('h', m, ti)])

        def dump(idx, half):
            if not dbg:
                return
            P.dma('sp', [(dbg_h[idx, half].rearrange("(k p) n -> p k n", p=128), hT[:])],
                  reads=[k for c in range(8) for k in hk(c)], semkey=('dbg', idx, half), is_out=True)

        def pool_layer(half):
            norm(C_NM0, 'pool')
            wv, wk = wget('pool')
            wp = wv.rearrange("p (g k n) -> p g k n", g=4, k=2)
            rk = [('rstd', t) for t in range(3)]
            if half == 1:
                for c in range(8):
                    P.op('dve', lambda h, c=c: h.scalar_tensor_tensor(
                        out=xs[:, c, :], in0=hT[:, c, 1024:1040], scalar=w32[:, C_NM0 + c:C_NM0 + c + 1],
                        in1=rstd[:, 1024:1040], op0=ALU.mult, op1=ALU.mult),
                        reads=[('h', c, 2), ('rstd', 2), 'w32'], writes=[('xs', c)])
                P.dma('sp', [(nps_x.rearrange("(k p) n -> p k n", p=128), xs[:])],
                      reads=[('xs', c) for c in range(8)], semkey='o_npsx', is_out=True)
                P.dma('sp', [(nps_prev[:, :, :], sp_prev.rearrange("(b j) d -> b j d", j=15)[:, 1:15, :])],
                      semkey='o_npsprev', is_out=True)
                for c in range(8):
                    g = c // 2
                    pb, pk = pbR.next()
                    for kc, kn in ((0, 128), (1, 112)):
                        off = C_AW + (kc * 4 + g) * 16
                        mm(pb[:, 0:16], RC_prev[0:kn, kc, c * 128:(c + 1) * 128], cf[0:kn, off:off + 16],
                           kc == 0, kc == 1, ['RC', 'cf'], [pk])
                    P.op('act', lambda h, c=c, pb=pb: h.copy(out=prevsum[:, c, :], in_=pb[:, 0:16]),
                         reads=[pk], writes=[('prevsum', c)])
            XB = [(RA_x, 'RA'), (RB_q, 'RB')]
            XbB = [(RC_bf[:, 0:2, :], ('RCx', 0)), (RC_bf[:, 2:4, :], ('RCx', 1))]

            def stageX(g):
                X, xkey = XB[g % 2]
                P.op('pool', lambda h: h.memset(X[:, :, 0:16], 0.0), writes=[xkey])
                for j in range(2):
                    c = 2 * g + j
                    P.op('dve', lambda h, c=c, j=j: h.scalar_tensor_tensor(
                        out=X[:, j, 32:1056], in0=hT[:, c, 0:1024], scalar=w32[:, C_NM0 + c:C_NM0 + c + 1],
                        in1=rstd[:, 0:1024], op0=ALU.mult, op1=ALU.mult),
                        reads=hk(c) + rk + ['w32'], writes=[xkey])
                    if half == 0:
                        P.op('dve', lambda h, c=c, j=j: h.scalar_tensor_tensor(
                            out=X[:, j, 16:32], in0=hT[:, c, 1024:1040], scalar=w32[:, C_NM0 + c:C_NM0 + c + 1],
                            in1=rstd[:, 1024:1040], op0=ALU.mult, op1=ALU.mult),
                            reads=hk(c) + rk + ['w32'], writes=[xkey])
                    else:
                        P.op('dve', lambda h, c=c, j=j: h.tensor_copy(out=X[:, j, 16:32], in_=halo[:, c, :]),
                             reads=[('halo', c)], writes=[xkey])

            stageX(0)
            for g in range(4):
                w = WINS[g]
                X, xkey = XB[g % 2]
                Xb, xbkey = XbB[g % 2]
                P.op('act', lambda h, X=X, Xb=Xb: h.copy(out=Xb, in_=X[:, :, 16:1056]), reads=[xkey],
                     writes=[xbkey] + (['RC'] if g < 2 else []))
                if g < 3:
                    stageX(g + 1)
                for j in range(2):
                    for (t0, tn) in ((0, 342), (342, 341), (683, 341)):
                        pb, pk = pbR.next()
                        for k in range(w):
                            mm(pb[:, 0:tn], idw[:, 2 * g + (1 if k else 0), :],
                               Xb[:, j, 16 + t0 - k:16 + t0 - k + tn], k == 0, k == w - 1,
                               [('idw', 2 * g + (1 if k else 0)), xbkey], [pk])
                        P.op('act', lambda h, pb=pb, j=j, t0=t0, tn=tn: h.copy(out=RB_diff[:, j, t0:t0 + tn], in_=pb[:, 0:tn]),
                             reads=[pk], writes=['RBd'])
                src, skey = X, xkey
                if half == 0:
                    bufs = [(P2, 'P2'), (Q2, 'Q2')]
                    sh = 1
                    bi = 0
                    while sh < w:
                        dst, dkey = bufs[bi % 2]
                        lo = 2 * sh - 1
                        P.op('dve', lambda h, dst=dst, src=src, sh=sh, lo=lo: h.tensor_tensor(
                            out=dst[:, :, lo:32], in0=src[:, :, lo:32], in1=src[:, :, lo - sh:32 - sh], op=ALU.add),
                            reads=[skey], writes=[dkey])
                        src, skey = dst, dkey
                        sh *= 2
                        bi += 1
                if half == 0:
                    tt, tk = affR.next()
                    tv = tt[:, 0, 0:32].rearrange("p (j n) -> p j n", j=2)
                    ic = cf[:, C_INVC + g * 16:C_INVC + (g + 1) * 16].unsqueeze(1).to_broadcast([128, 2, 16])
                    P.op('dve', lambda h, src=src, tv=tv, ic=ic: h.tensor_tensor(
                        out=tv, in0=src[:, :, 16:32], in1=ic, op=ALU.mult), reads=[skey, 'cf'], writes=[tk])
                    P.op('dve', lambda h, tv=tv, X=X: h.tensor_tensor(
                        out=RB_diff[:, :, 1024:1040], in0=tv, in1=X[:, :, 16:32], op=ALU.subtract),
                        reads=[tk, xkey], writes=['RBd'])
                    for j in range(2):
                        c = 2 * g + j
                        P.op('act', lambda h, c=c, j=j, X=X: h.copy(out=halo[:, c, :], in_=X[:, j, 1040:1056]),
                             reads=[xkey], writes=[('halo', c)])
                else:
                    for j in range(2):
                        c = 2 * g + j
                        tt, tk = affR.next()
                        P.op('dve', lambda h, c=c, tt=tt: h.tensor_tensor(
                            out=tt[:, 0, 0:16], in0=prevsum[:, c, :], in1=xs[:, c, :], op=ALU.add),
                            reads=[('prevsum', c), ('xs', c)], writes=[tk])
                        P.op('dve', lambda h, c=c, j=j, tt=tt, w=w: h.scalar_tensor_tensor(
                            out=RB_diff[:, j, 1024:1040], in0=tt[:, 0, 0:16], scalar=1.0 / w, in1=xs[:, c, :],
                            op0=ALU.mult, op1=ALU.subtract), reads=[tk, ('xs', c)], writes=['RBd'])
                        P.dma('sp', [(npp[c * 128:(c + 1) * 128, :], X[:, j, 1040:1056])], reads=[xkey],
                              semkey=('o_npp', c), is_out=True)
                for m in range(2):
                    cm = 2 * g + m
                    for ti, (t0, tn) in enumerate(TBS):
                        pb, pk = pbR.next()
                        for kc in range(2):
                            mm(pb[:, 0:tn], wp[:, g, kc, m * 128:(m + 1) * 128], RB_diff[:, kc, t0:t0 + tn],
                               kc == 0, kc == 1, [wk, 'RBd'], [pk])
                        hadd(cm, ti, t0, tn, pb, pk, scale_ap=cf[:, C_PS + cm:C_PS + cm + 1])

        def ffn(l):
            norm(C_NF0 if l == 0 else C_NF1, 'bf16')
            for fh in range(2):
                ntile = 6 if fh == 0 else 5
                nk = 2 * ntile
                for j in range(ntile):
                    wv, wk = wget('gu')
                    for fc in range(2):
                        hv, hkey = hid(2 * j + fc)
                        for ti, (t0, tn) in enumerate(TBS):
                            pg, pgk = pbR.next()
                            pu, puk = pbR.next()
                            for k in range(8):
                                mm(pg[:, 0:tn], wv[:, k, fc * 128:(fc + 1) * 128], xn[:, k, t0:t0 + tn],
                                   k == 0, k == 7, [wk, ('xn', k)], [pgk])
                            for k in range(8):
                                mm(pu[:, 0:tn], wv[:, k, 256 + fc * 128:256 + (fc + 1) * 128], xn[:, k, t0:t0 + tn],
                                   k == 0, k == 7, [wk, ('xn', k)], [puk])
                            sg, sgk = sgR.next()
                            P.op('act', lambda h, sg=sg, pg=pg, tn=tn: h.activation(out=sg[:, 0:tn], in_=pg[:, 0:tn], func=ACT.Silu),
                                 reads=[pgk], writes=[sgk])
                            P.op('dve', lambda h, hv=hv, pu=pu, sg=sg, t0=t0, tn=tn: h.tensor_tensor(
                                out=hv[:, t0:t0 + tn], in0=pu[:, 0:tn], in1=sg[:, 0:tn], op=ALU.mult),
                                reads=[puk, sgk], writes=[hkey])
                for mt in range(4):
                    wv, wk = wget('d')
                    for mmi in range(2):
                        m = 2 * mt + mmi
                        for ti, (t0, tn) in enumerate(TBS):
                            pb, pk = pbR.next()
                            for kc in range(nk):
                                hv, hkey = hid(kc)
                                mm(pb[:, 0:tn], wv[:, kc, mmi * 128:(mmi + 1) * 128], hv[:, t0:t0 + tn],
                                   kc == 0, kc == nk - 1, [wk, hkey], [pk])
                            hadd(m, ti, t0, tn, pb, pk)

        def gn_norm(po, pok, n):
            return gn_norm_b(*gn_norm_a(po, pok, n))

        def gn_norm_a(po, pok, n):
            stt, stk = stR.next()
            P.op('dve', lambda hh: hh.bn_stats(out=stt[0:n, 0:6], in_=po[0:n, :]), reads=[pok], writes=[stk])
            P.op('dve', lambda hh: hh.bn_aggr(out=stt[0:n, 6:8], in_=stt[0:n, 0:6]), reads=[stk], writes=[stk])
            P.op('act', lambda hh: hh.activation(out=stt[0:n, 8:9], in_=stt[0:n, 7:8], func=ACT.Sqrt, bias=epsb[0:n, 1:2], scale=1.0),
                 reads=[stk, 'epsb'], writes=[stk])
            return po, pok, n, stt, stk

        def gn_norm_b(po, pok, n, stt, stk):
            P.op('dve', lambda hh: hh.reciprocal(out=stt[0:n, 8:9], in_=stt[0:n, 8:9]), reads=[stk], writes=[stk])
            on, onk = onR.next()
            P.op('dve', lambda hh: hh.scalar_tensor_tensor(out=stt[0:n, 9:10], in0=stt[0:n, 6:7], scalar=-1.0,
                                                           in1=stt[0:n, 8:9], op0=ALU.mult, op1=ALU.mult),
                 reads=[stk], writes=[stk])
            P.op('act', lambda hh: hh.activation(out=on[0:n, :], in_=po[0:n, :], func=ACT.Identity,
                                                 bias=stt[0:n, 9:10], scale=stt[0:n, 8:9]),
                 reads=[pok, stk], writes=[onk])
            return on, onk

        def gn_gate(on, onk, n, c0, h, gsel=0):
            gbuf, gkey = GB[gsel]
            pg, pgk = pbR.next()
            pv = pg[:].bitcast(BF16)[:, 0:512].rearrange("p (e n) -> p e n", e=4)
            for e4 in range(4):
                tr(pv[:, e4, 0:n], on[0:n, e4 * 128:(e4 + 1) * 128], identb[0:n, 0:n], [onk, 'identb'], [pgk])
            af, afk = affR.next()
            gw = cf[:, C_GNW + h * 4:C_GNW + h * 4 + 4].unsqueeze(2).to_broadcast([128, 4, n])
            gb = cf[:, C_GNB + h * 4:C_GNB + h * 4 + 4].unsqueeze(2).to_broadcast([128, 4, n])
            P.op('dve', lambda hh: hh.tensor_tensor(out=af[:, :, 0:n], in0=pv[:, :, 0:n], in1=gw, op=ALU.mult),
                 reads=[pgk, 'cf'], writes=[afk])
            P.op('pool', lambda hh: hh.tensor_tensor(out=af[:, :, 0:n], in0=af[:, :, 0:n], in1=gb, op=ALU.add),
                 reads=[afk, 'cf'], writes=[afk])
            P.op('pool', lambda hh: hh.tensor_tensor(out=gbuf[:, :, c0:c0 + n], in0=af[:, :, 0:n],
                                                     in1=gbuf[:, :, c0:c0 + n], op=ALU.mult),
                 reads=[afk, gkey], writes=[(gkey + 'g', c0)])

        def chunk_prep(h, idx, n, c0, first):
            ps, psk = pbR.next()
            for c in range(2):
                mm(ps[0:n, 0:n], RA_bf[:, 2 + c, c0:c0 + n], RA_bf[:, c, c0:c0 + n], c == 0, c == 1, ['RA'], [psk])
            P.op('dve', lambda hh: hh.tensor_tensor(out=pmA[0:n, idx, 0:n], in0=ps[0:n, 0:n],
                                                    in1=cf[0:n, C_MASK + h * 128:C_MASK + h * 128 + n], op=ALU.mult),
                 reads=[psk, 'cf'], writes=[('pmA', idx)])
            pt, ptk = pbR.next()
            psT = pt[:].bitcast(BF16)[:, 0:256]
            ptv = psT.rearrange("p (c n) -> p c n", c=2)
            for c in range(2):
                tr(ptv[0:n, c, :], RA_bf[:, 2 + c, c0:c0 + n], identb[:, :], ['RA', 'identb'], [ptk])
            kcol = (C_KDEC if n == 128 else C_KDECM) + h
            P.op('act', lambda hh: hh.activation(out=kdA[0:n, idx, :], in_=psT[0:n, :], func=ACT.Copy,
                                                 scale=cf[0:n, kcol:kcol + 1]),
                 reads=[ptk, 'cf'], writes=[('kdA', idx)])
            if not first:
                cr = cf[:, C_CROSS + h * 128:C_CROSS + h * 128 + n].unsqueeze(1).to_broadcast([128, 2, n])
                P.op('pool', lambda hh: hh.tensor_tensor(out=qcA[:, idx, :, 0:n], in0=RA_bf[:, 0:2, c0:c0 + n], in1=cr, op=ALU.mult),
                     reads=['RA', 'cf'], writes=[('qcA', idx)])

        def chunk_state(h, idx, t, n, first, sidx):
            for c in range(2):
                mm(psU[:, c, :], kdA[0:n, idx, c * 128:(c + 1) * 128], RB_v[0:n, t, :], True, True,
                   [('kdA', idx), 'RB'], ['psU'])
            if first:
                P.op('dve', lambda hh: hh.tensor_copy(out=S32w[:], in_=psU[:]), reads=['psU'], writes=['S32w'])
            else:
                P.op('dve', lambda hh: hh.scalar_tensor_tensor(out=S32w[:], in0=S32w[:], scalar=gam128[h], in1=psU[:],
                                                               op0=ALU.mult, op1=ALU.add),
                     reads=['psU', 'S32w'], writes=['S32w'])
            scur = SbfR[sidx % 3]
            P.op('act', lambda hh: hh.copy(out=scur[:], in_=S32w[:]), reads=['S32w'], writes=[('Sbf', sidx % 3)])

        def chunk_intra(idx, t, n, first):
            po, pok = pbR.next()
            mm(po[0:n, :], pmA[0:n, idx, 0:n], RB_v[0:n, t, :], True, first, [('pmA', idx), 'RB'], [pok])
            return po, pok

        def chunk_cross(idx, n, first, sidx, po, pok):
            if not first:
                sprev = SbfR[(sidx - 1) % 3]
                for c in range(2):
                    mm(po[0:n, :], qcA[:, idx, c, 0:n], sprev[:, c, :], False, c == 1,
                       [('qcA', idx), ('Sbf', (sidx - 1) % 3)], [pok])
            return gn_norm_a(po, pok, n)

        def sample_prep(h):
            pt, ptk = pbR.next()
            psT = pt[:].bitcast(BF16)[:, 0:256]
            ptv = psT.rearrange("p (c n) -> p c n", c=2)
            for c in range(2):
                tr(ptv[0:16, c, :], RA_bf[:, 2 + c, 1024:1040], identb[:, :], ['RA', 'identb'], [ptk])
            P.op('act', lambda hh: hh.copy(out=kdS[0:16, :], in_=psT[0:16, :]), reads=[ptk], writes=['kdS'])
            P.op('act', lambda hh: hh.copy(out=vS[0:16, :], in_=RB_v[0:16, 8, :]), reads=['RB'], writes=['vS'])
            P.op('act', lambda hh: hh.copy(out=qsf[:], in_=RA_bf[:, 0:2, 1024:1040]), reads=['RA'], writes=['qsf'])
            d16 = cf[:, C_D16:C_D16 + 256].rearrange("p (a b) -> p a b", a=16)
            for c in range(2):
                P.op('dve', lambda hh, c=c: hh.tensor_tensor(out=qm[:, c, :, :],
                                                             in0=qsf[:, c, :].unsqueeze(2).to_broadcast([128, 16, 16]),
                                                             in1=d16, op=ALU.mult), reads=['qsf', 'cf'], writes=['qm'])

        kmcur = [None, None]

        def sample_unit(h, b):
            if b % 4 == 0:
                km, kmk = kmaskR.next()
                kmcur[0], kmcur[1] = km, kmk
                i16 = cf[0:16, C_I16 + b:C_I16 + b + 4].unsqueeze(2).to_broadcast([16, 4, 256])
                P.op('dve', lambda hh, km=km, i16=i16: hh.tensor_tensor(
                    out=km[0:16, :, :], in0=kdS[0:16, :].unsqueeze(1).to_broadcast([16, 4, 256]), in1=i16, op=ALU.mult),
                    reads=['kdS', 'cf'], writes=[kmk])
            km, kmk = kmcur

            def s_load(bb):
                u = h * 16 + bb
                t_, k_ = sring.tiles[u % 4], ('sr', u % 4)
                P.dma('sp', [(t_[:], sret[bb, h].rearrange("c p e -> p c e"))], writes=[k_])
            if b == 0:
                s_load(0)
                s_load(1)
            if b + 2 < 16:
                s_load(b + 2)
            u = h * 16 + b
            sl, slk = sring.tiles[u % 4], ('sr', u % 4)
            for c in range(2):
                mm(psU[:, c, :], km[0:16, b % 4, c * 128:(c + 1) * 128], vS[0:16, :], True, True,
                   [kmk, 'vS'], ['psU'])
            P.op('dve', lambda hh, sl=sl: hh.scalar_tensor_tensor(out=sl[:], in0=sl[:], scalar=gam[h], in1=psU[:],
                                                                  op0=ALU.mult, op1=ALU.add),
                 reads=['psU', slk], writes=[slk])
            P.dma('sp', [(nrs[b, h].rearrange("c p e -> p c e"), sl[:])], reads=[slk], is_out=True)
            if spend:
                sample_q(*spend.pop(0))
            spend.append((b, sl, slk))

        spend = []
        sqpend = []

        def sample_q(b, sl, slk):
            sb_, sbk = snbR[b % 2], ('snb', b % 2)
            if sqpend:
                sample_o(*sqpend.pop(0))
            P.op('act', lambda hh, sl=sl, sb_=sb_: hh.copy(out=sb_[:], in_=sl[:]), reads=[slk], writes=[sbk])
            sqpend.append((b, sb_, sbk))

        def sample_o(b, sb_, sbk):
            po, pok = pbR.next()
            for c in range(2):
                mm(po[0:16, :], qm[:, c, b, :], sb_[:, c, :], c == 0, c == 1, ['qm', sbk], [pok])
            if b == 0:
                P.op('dve', lambda hh, po=po: hh.tensor_copy(out=oacc[0:16, :], in_=po[0:16, :]), reads=[pok], writes=['oacc'])
            else:
                P.op('dve', lambda hh, po=po: hh.tensor_tensor(out=oacc[0:16, :], in0=po[0:16, :], in1=oacc[0:16, :], op=ALU.add),
                     reads=[pok, 'oacc'], writes=['oacc'])

        def sample_finish(h, gsel):
            while spend:
                sample_q(*spend.pop(0))
            while sqpend:
                sample_o(*sqpend.pop(0))
            on, onk = gn_norm(oacc, 'oacc', 16)
            gn_gate(on, onk, 16, 1024, h, gsel)

        def ret_layer(half):
            norm(C_NM1, 'bf16')
            step = [0]

            def proj(h, gsel, tick):
                gbuf, gkey = GB[gsel]
                wv, wk = wget('in')
                for qk in range(2):
                    for ti, (t0, tn) in enumerate(TBS):
                        rs = step[0] % 2
                        step[0] += 1
                        rot = rotS[rs]
                        rkk = ['rot%d_%d' % (rs, i) for i in range(4)]
                        pa, pak = pbR.next()
                        pbb, pbk = pbR.next()
                        for k in range(8):
                            mm(pa[:, 0:tn], wv[:, k, (2 * qk) * 128:(2 * qk + 1) * 128], xn[:, k, t0:t0 + tn],
                               k == 0, k == 7, [wk, ('xn', k)], [pak])
                        for k in range(8):
                            mm(pbb[:, 0:tn], wv[:, k, (2 * qk + 1) * 128:(2 * qk + 2) * 128], xn[:, k, t0:t0 + tn],
                               k == 0, k == 7, [wk, ('xn', k)], [pbk])
                        co = cs[:, 0, t0:t0 + tn]
                        si = cs[:, 1, t0:t0 + tn]
                        P.op('dve', lambda hh, pa=pa, co=co, tn=tn, rot=rot: hh.tensor_tensor(out=rot[0][:, 0:tn], in0=pa[:, 0:tn], in1=co, op=ALU.mult),
                             reads=[pak, 'cs'], writes=[rkk[0]])
                        P.op('dve', lambda hh, pbb=pbb, si=si, tn=tn, rot=rot: hh.tensor_tensor(out=rot[1][:, 0:tn], in0=pbb[:, 0:tn], in1=si, op=ALU.mult),
                             reads=[pbk, 'cs'], writes=[rkk[1]])
                        P.op('dve', lambda hh, pa=pa, si=si, tn=tn, rot=rot: hh.tensor_tensor(out=rot[2][:, 0:tn], in0=pa[:, 0:tn], in1=si, op=ALU.mult),
                             reads=[pak, 'cs'], writes=[rkk[2]])
                        P.op('dve', lambda hh, pbb=pbb, co=co, tn=tn, rot=rot: hh.tensor_tensor(out=rot[3][:, 0:tn], in0=pbb[:, 0:tn], in1=co, op=ALU.mult),
                             reads=[pbk, 'cs'], writes=[rkk[3]])
                        P.op('pool', lambda hh, qk=qk, t0=t0, tn=tn, rot=rot: hh.tensor_tensor(out=RA_bf[:, 2 * qk, t0:t0 + tn], in0=rot[0][:, 0:tn],
                                                                                    in1=rot[1][:, 0:tn], op=ALU.subtract),
                             reads=[rkk[0], rkk[1]], writes=['RA'])
                        P.op('pool', lambda hh, qk=qk, t0=t0, tn=tn, rot=rot: hh.tensor_tensor(out=RA_bf[:, 2 * qk + 1, t0:t0 + tn], in0=rot[2][:, 0:tn],
                                                                                    in1=rot[3][:, 0:tn], op=ALU.add),
                             reads=[rkk[2], rkk[3]], writes=['RA'])
                        tick()
                wv, wk = wget('in')
                for t in range(9):
                    n = 128 if t < 8 else 16
                    c0 = t * 128
                    pb, pk = pbR.next()
                    for k in range(8):
                        mm(pb[0:n, :], xn[:, k, c0:c0 + n], wv[:, k, :], k == 0, k == 7, [wk, ('xn', k)], [pk])
                    P.op('act', lambda hh, pb=pb, t=t, n=n: hh.copy(out=RB_v[0:n, t, :], in_=pb[0:n, :]),
                         reads=[pk], writes=['RB'])
                    tick()
                wv, wk = wget('in')
                for m in range(4):
                    for ti, (t0, tn) in enumerate(TBS):
                        pb, pk = pbR.next()
                        for k in range(8):
                            mm(pb[:, 0:tn], wv[:, k, m * 128:(m + 1) * 128], xn[:, k, t0:t0 + tn],
                               k == 0, k == 7, [wk, ('xn', k)], [pk])
                        P.op('act', lambda hh, pb=pb, m=m, t0=t0, tn=tn: hh.activation(out=gbuf[:, m, t0:t0 + tn], in_=pb[:, 0:tn], func=ACT.Silu),
                             reads=[pk], writes=[gkey])
                        tick()

            def chunks(h, gsel, tiles):
                if half == 1:
                    P.dma('sp', [(S32w[:], sscr[h].rearrange("c p e -> p c e"))], reads=[('nrpd', h)], writes=['S32w'],
                          semkey=('nrp_ld', h))
                    P.op('act', lambda hh: hh.copy(out=SbfR[0][:], in_=S32w[:]), reads=['S32w'], writes=[('Sbf', 0)])
                for idx, (t, n, c0) in enumerate(tiles):
                    chunk_prep(h, idx, n, c0, half == 0 and idx == 0)
                pend = []
                cpend = []
                for idx, (t, n, c0) in enumerate(tiles):
                    sidx = idx if half == 0 else idx + 1
                    first = (half == 0 and idx == 0)
                    po, pok = chunk_intra(idx, t, n, first)
                    st_ = chunk_cross(*cpend.pop(0)) if cpend else None
                    chunk_state(h, idx, t, n, first, sidx)
                    if st_ is not None:
                        on, onk = gn_norm_b(*st_)
                        pend.append((on, onk) + cmeta.pop(0))
                    else:
                        cmeta = []
                    cpend.append((idx, n, first, sidx, po, pok))
                    cmeta.append((n, c0))
                    if len(pend) > 1:
                        gn_gate(*pend.pop(0), h, gsel)
                while cpend:
                    on, onk = gn_norm_b(*chunk_cross(*cpend.pop(0)))
                    pend.append((on, onk) + cmeta.pop(0))
                while pend:
                    gn_gate(*pend.pop(0), h, gsel)
                P.dma('sp', [((sscr if half == 0 else nrp)[h].rearrange("c p e -> p c e"), S32w[:])], reads=['S32w'],
                      writes=[('nrpd', h)], semkey=('o_nrp', h), is_out=True)

            def outproj(h, gsel, tiles, tick=None, before_special=None):
                gbuf, gkey = GB[gsel]
                wv, wk = wget('out')
                allc = [c0 for (_, _, c0) in tiles] + ([1024] if half == 1 else [])
                for ti, (t0, tn) in enumerate(TBS):
                    if ti == 2 and before_special is not None:
                        before_special()
                    gk = [gkey] + [(gkey + 'g', c0) for c0 in allc
                                   if c0 < t0 + tn and c0 + (16 if c0 == 1024 else 128) > t0]
                    for m in range(8):
                        pb, pk = pbR.next()
                        for e4 in range(4):
                            mm(pb[:, 0:tn], wv[:, e4, m * 128:(m + 1) * 128], gbuf[:, e4, t0:t0 + tn],
                               e4 == 0, e4 == 3, [wk] + gk, [pk])
                        hadd(m, ti, t0, tn, pb, pk)
                        if tick is not None and ti < 2:
                            tick()

            if half == 0:
                tiles = [(8, 16, 1024)] + [(t, 128, t * 128) for t in range(8)]
                for h in range(4):
                    proj(h, 0, lambda: None)
                    chunks(h, 0, tiles)
                    outproj(h, 0, tiles)
            else:
                tiles = [(t, 128, t * 128) for t in range(8)]
                def make_tick(hp, nsteps):
                    cnt = [0, 0]

                    def tick():
                        cnt[0] += 1
                        while cnt[1] < 16 and cnt[1] * nsteps < cnt[0] * 16:
                            sample_unit(hp, cnt[1])
                            cnt[1] += 1
                    return tick, cnt
                for h in range(4):
                    if h == 0:
                        proj(h, 0, lambda: None)
                    else:
                        tick, cnt = make_tick(h - 1, 27 + 16)
                        proj(h, h % 2, tick)

                        def fin(h=h, cnt=cnt):
                            assert cnt[1] == 16, cnt
                            sample_finish(h - 1, (h - 1) % 2)
                        outproj(h - 1, (h - 1) % 2, tiles, tick=tick, before_special=fin)
                    chunks(h, h % 2, tiles)
                    sample_prep(h)
                tick, cnt = make_tick(3, 16)

                def fin3(cnt=cnt):
                    assert cnt[1] == 16, cnt
                    sample_finish(3, 1)
                outproj(3, 1, tiles, tick=tick, before_special=fin3)

        for half in range(2):
            c0 = half * NT
            if half == 1:
                P.dma('sp', [(RC_prev[:, 0, :], sp_prev[0:128, :]), (RC_prev[0:112, 1, :], sp_prev[128:240, :])],
                      writes=['RC'])
            xv = xT.rearrange("(k p) n -> p k n", p=128)
            for c in range(8):
                P.dma('sp', [(hT[:, c, :], xv[:, c, c0:c0 + NT])], writes=hk(c), semkey=('hload', c))
            P.dma('sp', [(cs[:, 0, :], csd[0, :, c0:c0 + NT]), (cs[:, 1, :], csd[1, :, c0:c0 + NT])], writes=['cs'])
            pool_layer(half)
            dump(0, half)
            ffn(0)
            dump(1, half)
            ret_layer(half)
            dump(2, half)
            ffn(1)
            dump(3, half)
            norm(C_NFIN, 'final')
            yv = yT.rearrange("(k p) n -> p k n", p=128)
            for c in range(8):
                P.dma('sp', [(yv[:, c, c0:c0 + NT], hT[:, c, :])], reads=hk(c), semkey=('o_y', half, c), is_out=True)
        assert wstate['cur'] == len(wl), (wstate, len(wl))
        P.finish()
        stats = P.emit()
    return nc, stats


def _const_tables():
    lg = _gammas()
    cfc = np.zeros((128, NCF), np.float32)
    idx = np.arange(128, dtype=np.float32)
    for h in range(4):
        cfc[:, C_KDEC + h] = np.exp((np.float32(127.0) - idx) * lg[h])
        cfc[:16, C_KDECM + h] = np.exp((np.float32(15.0) - idx[:16]) * lg[h])
        rel = idx[None, :] - idx[:, None]
        m = np.where(rel >= 0, np.exp(np.maximum(rel, 0) * lg[h]), 0.0).astype(np.float32) * np.float32(1.0 / 16.0)
        cfc[:, C_MASK + h * 128:C_MASK + (h + 1) * 128] = m
        cfc[:, C_CROSS + h * 128:C_CROSS + (h + 1) * 128] = (np.exp((idx + 1.0) * lg[h]) * np.float32(1.0 / 16.0))[None, :]
    for g, w in enumerate(WINS):
        t = np.arange(16)
        cfc[:, C_INVC + g * 16:C_INVC + (g + 1) * 16] = (1.0 / np.minimum(t + 1, w)).astype(np.float32)[None, :]
    cfc[:16, C_I16:C_I16 + 16] = np.eye(16, dtype=np.float32)
    cfc[:, C_D16:C_D16 + 256] = (np.eye(16, dtype=np.float32) / 16.0).reshape(1, 256)
    for kc in range(2):
        for g, w in enumerate(WINS):
            for r in range(128):
                R = kc * 128 + r
                if R >= 240:
                    continue
                b, j = divmod(R, 15)
                if j >= 16 - w:
                    cfc[r, C_AW + (kc * 4 + g) * 16 + b] = 1.0
    cfc[:, C_ID:C_ID + 128] = np.eye(128, dtype=np.float32)
    theta = (1.0 / (np.float32(10000.0) ** np.linspace(0.0, 1.0, 128, dtype=np.float32))).astype(np.float32)
    pos = np.concatenate([np.arange(16, 1040), np.arange(0, 16), np.arange(1040, 2064), np.full(16, 16384)]).astype(np.float32)
    ang = (pos[None, :] * theta[:, None]).astype(np.float32)
    cs = np.stack([np.cos(ang), np.sin(ang)]).astype(np.float32)
    return cfc, cs


def _vec8(v):
    return np.ascontiguousarray(v.reshape(-1, 128).T)


_CACHE = {}


def kernel(x_prompt, x_sample, state_pool, state_ret, meta_tokens, norm_mix, norm_ffn, norm_final,
           w_pool, pool_scale, w_ret_in, ret_gn_w, ret_gn_b, w_ret_out, w_ffn_gate, w_ffn_up, w_ffn_down, _dbg=False):
    f = lambda a: np.asarray(a, dtype=np.float32)
    x_prompt, x_sample, state_pool, state_ret, meta_tokens = map(f, (x_prompt, x_sample, state_pool, state_ret, meta_tokens))
    norm_mix, norm_ffn, norm_final, w_pool, pool_scale = map(f, (norm_mix, norm_ffn, norm_final, w_pool, pool_scale))
    w_ret_in, ret_gn_w, ret_gn_b, w_ret_out = map(f, (w_ret_in, ret_gn_w, ret_gn_b, w_ret_out))
    w_ffn_gate, w_ffn_up, w_ffn_down = map(f, (w_ffn_gate, w_ffn_up, w_ffn_down))

    key = bool(_dbg)
    if key not in _CACHE:
        _CACHE[key] = build_program(dbg=_dbg)
    nc, _ = _CACHE[key]

    cfc, cs = _const_tables()
    cfc[:, C_NM0:C_NM0 + 8] = _vec8(norm_mix[0])
    cfc[:, C_NM1:C_NM1 + 8] = _vec8(norm_mix[1])
    cfc[:, C_NF0:C_NF0 + 8] = _vec8(norm_ffn[0])
    cfc[:, C_NF1:C_NF1 + 8] = _vec8(norm_ffn[1])
    cfc[:, C_NFIN:C_NFIN + 8] = _vec8(norm_final)
    cfc[:, C_PS:C_PS + 8] = _vec8(pool_scale[0])
    cfc[:, C_GNW:C_GNW + 16] = _vec8(ret_gn_w[0])
    cfc[:, C_GNB:C_GNB + 16] = _vec8(ret_gn_b[0])

    wpool_h = np.ascontiguousarray(w_pool[0].reshape(4, 2, 128, 256).transpose(2, 0, 1, 3).reshape(128, 2048))
    w_gu = np.empty((2, 11, 1024, 512), np.float32)
    for l in range(2):
        w_gu[l, :, :, 0:256] = w_ffn_gate[l].reshape(1024, 11, 256).transpose(1, 0, 2)
        w_gu[l, :, :, 256:512] = w_ffn_up[l].reshape(1024, 11, 256).transpose(1, 0, 2)
    w_d0 = np.ascontiguousarray(w_ffn_down[:, 0:1536, :].reshape(2, 1536, 4, 256).transpose(0, 2, 1, 3))
    w_d1 = np.ascontiguousarray(w_ffn_down[:, 1536:2816, :].reshape(2, 1280, 4, 256).transpose(0, 2, 1, 3))
    wi = w_ret_in[0]
    w_in = np.empty((4, 3, 1024, 512), np.float32)
    for h in range(4):
        q = wi[:, h * 256:(h + 1) * 256]
        k = wi[:, 1024 + h * 256:1024 + (h + 1) * 256]
        w_in[h, 0, :, 0:128] = q[:, 0::2]
        w_in[h, 0, :, 128:256] = q[:, 1::2]
        w_in[h, 0, :, 256:384] = k[:, 0::2]
        w_in[h, 0, :, 384:512] = k[:, 1::2]
        w_in[h, 1] = wi[:, 2048 + h * 512:2048 + (h + 1) * 512]
        w_in[h, 2] = wi[:, 4096 + h * 512:4096 + (h + 1) * 512]
    w_out = np.ascontiguousarray(w_ret_out[0].reshape(4, 512, 1024))

    def _pm(a, K):
        lead, N = a.shape[:-2], a.shape[-1]
        return np.ascontiguousarray(a.reshape(*lead, K, 128, N).swapaxes(-3, -2).reshape(*lead, 128, K * N))
    w_gu_pm, w_d0_pm, w_d1_pm = _pm(w_gu, 8), _pm(w_d0, 12), _pm(w_d1, 10)
    w_in_pm, w_out_pm = _pm(w_in, 8), _pm(w_out, 4)

    in_maps = []
    for b in range(NCORES):
        xs_ = x_sample[b * 16:(b + 1) * 16, 0, :]
        cols = np.concatenate([x_prompt[b, 0:1024], meta_tokens, x_prompt[b, 1024:2048], xs_], axis=0)
        sr = state_ret[0, b * 16:(b + 1) * 16].reshape(16, 4, 128, 2, 512).transpose(0, 1, 3, 2, 4)
        in_maps.append({
            "xT": np.ascontiguousarray(cols.T),
            "cs": cs, "cf": cfc,
            "sp_prev": np.ascontiguousarray(state_pool[0, b * 16:(b + 1) * 16].reshape(240, 1024)),
            "sret": np.ascontiguousarray(sr),
            "wpool": wpool_h, "w_gu": w_gu_pm, "w_d0": w_d0_pm, "w_d1": w_d1_pm, "w_in": w_in_pm, "w_out": w_out_pm,
        })
    res = run_bass_kernel_spmd(nc, in_maps, core_ids=list(range(NCORES)))
    R = res.results

    y_prompt = np.empty((8, 2048, 1024), np.float32)
    y_sample = np.empty((128, 1, 1024), np.float32)
    npp_o = np.empty((1, 8, 15, 1024), np.float32)
    nps_o = np.empty((1, 128, 15, 1024), np.float32)
    nrp_o = np.empty((1, 8, 4, 256, 512), np.float32)
    nrs_o = np.empty((1, 128, 4, 256, 512), np.float32)
    for b in range(NCORES):
        r = R[b]
        yt = r["yT"]
        y_prompt[b, 0:1024] = yt[:, 0:1024].T
        y_prompt[b, 1024:2048] = yt[:, NT:NT + 1024].T
        y_sample[b * 16:(b + 1) * 16, 0, :] = yt[:, NT + 1024:NT + 1040].T
        npp_o[0, b] = r["npp"][:, 1:16].T
        nps_o[0, b * 16:(b + 1) * 16, 0:14] = r["nps_prev"]
        nps_o[0, b * 16:(b + 1) * 16, 14] = r["nps_x"].T
        nrp_o[0, b] = r["nrp"].transpose(0, 2, 1, 3).reshape(4, 256, 512)
        nrs_o[0, b * 16:(b + 1) * 16] = r["nrs"].transpose(0, 1, 3, 2, 4).reshape(16, 4, 256, 512)
    outs = (y_prompt, y_sample, npp_o, nps_o, nrp_o, nrs_o)
    if _dbg:
        return outs, R
    return outs
```

```python
import contextlib
import numpy as np
import concourse.bass as bass
import concourse.mybir as mybir
from concourse.bass_utils import run_bass_kernel_spmd

F32 = mybir.dt.float32
BF16 = mybir.dt.bfloat16
ACT = mybir.ActivationFunctionType
ALU = mybir.AluOpType

ENGS = ('pe', 'act', 'dve', 'pool', 'sp')
NCORES = 8
NT = 1040
TBS = ((0, 347), (347, 347), (694, 346))
EPS = 1e-6
WINS = (2, 4, 8, 16)

C_NM0, C_NM1, C_NF0, C_NF1, C_NFIN, C_PS, C_GNW, C_GNB = 0, 8, 16, 24, 32, 40, 48, 64
C_KDEC, C_KDECM, C_MASK, C_CROSS, C_INVC, C_I16, C_D16, C_AW, C_ID = 80, 84, 88, 600, 1112, 1176, 1192, 1448, 1576
NCF = 1704


class Op:
    __slots__ = ('eng', 'fn', 'deps', 'signal', 'sem', 'sigval', 'is_dma')

    def __init__(self, eng, fn):
        self.eng = eng
        self.fn = fn
        self.deps = ()
        self.signal = False
        self.sem = None
        self.sigval = 0
        self.is_dma = False


class Prog:
    def __init__(self, nc, stack):
        self.nc = nc
        self.stack = stack
        self.ops = {e: [] for e in ENGS}
        self.sems = {e: stack.enter_context(nc.semaphore("s_" + e)) for e in ENGS if e != 'sp'}
        self.last_w = {}
        self.readers = {}
        self.dma_sems = {}
        self.out_dmas = []

    def _mk(self, eng, fn, reads, writes):
        o = Op(eng, fn)
        deps = set()
        for k in reads:
            w = self.last_w.get(k)
            if w is not None:
                deps.add(w)
        for k in writes:
            w = self.last_w.get(k)
            if w is not None:
                deps.add(w)
            deps.update(self.readers.get(k, ()))
        o.deps = deps
        for d in deps:
            d.signal = True
        for k in writes:
            self.last_w[k] = o
            self.readers[k] = []
        for k in reads:
            self.readers.setdefault(k, []).append(o)
        self.ops[eng].append(o)
        return o

    def op(self, eng, fn, reads=(), writes=()):
        return self._mk(eng, fn, reads, writes)

    def dma(self, eng, pairs, reads=(), writes=(), semkey=None, is_out=False):
        if semkey is None:
            semkey = (tuple(writes) + tuple(reads))[0]
        ent = self.dma_sems.get(semkey)
        if ent is None:
            sem = self.stack.enter_context(self.nc.semaphore("d%d" % len(self.dma_sems)))
            ent = [sem, 0]
            self.dma_sems[semkey] = ent
        sem = ent[0]

        def fn(h, pairs=pairs, sem=sem):
            for (o_ap, i_ap) in pairs:
                h.dma_start(out=o_ap, in_=i_ap).then_inc(sem, 16)
            return None

        o = self._mk(eng, fn, reads, writes)
        o.is_dma = True
        ent[1] += 16 * len(pairs)
        o.sem = sem
        o.sigval = ent[1]
        if is_out:
            self.out_dmas.append(o)
        return o

    def finish(self):
        o = Op('sp', lambda h: None)
        o.deps = set(self.out_dmas)
        self.ops['sp'].append(o)

    def emit(self):
        nc = self.nc
        for e in ENGS:
            c = 0
            for o in self.ops[e]:
                if not o.is_dma and o.signal:
                    c += 1
                    o.sigval = c
                    o.sem = self.sems[e]
        stats = {}

        def run(e, h):
            seen = {}
            nw = 0
            for o in self.ops[e]:
                waits = {}
                for d in o.deps:
                    if (not d.is_dma) and d.eng == 'pe' and e == 'pe':
                        continue
                    if waits.get(d.sem, (0, 0))[0] < d.sigval:
                        waits[d.sem] = (d.sigval, d.sem)
                for v, sem in waits.values():
                    if seen.get(sem, 0) < v:
                        h.wait_ge(sem, v)
                        seen[sem] = v
                        nw += 1
                ins = o.fn(h)
                if (not o.is_dma) and o.signal:
                    ins.then_inc(o.sem, 1)
            stats[e] = (len(self.ops[e]), nw)

        with nc.Block() as block:
            @block.tensor
            def _(h):
                run('pe', h)

            @block.scalar
            def _(h):
                run('act', h)

            @block.vector
            def _(h):
                run('dve', h)

            @block.gpsimd
            def _(h):
                run('pool', h)

            @block.sync
            def _(h):
                run('sp', h)
        return stats


class Ring:
    def __init__(self, name, tiles):
        self.name = name
        self.tiles = tiles
        self.i = 0

    def next(self):
        i = self.i % len(self.tiles)
        self.i += 1
        return self.tiles[i], (self.name, i)


def _gammas():
    lg = np.log(np.float32(1.0) - np.float32(2.0) ** (-5.0 - np.arange(4, dtype=np.float32))).astype(np.float32)
    return lg


def build_program(dbg=False):
    nc = bass.Bass("TRN2", target_bir_lowering=False)

    def DI(name, shape):
        return nc.dram_tensor(name, list(shape), F32, kind="ExternalInput").ap()

    def DO(name, shape):
        return nc.dram_tensor(name, list(shape), F32, kind="ExternalOutput").ap()

    xT = DI("xT", [1024, 2 * NT])
    csd = DI("cs", [2, 128, 2 * NT])
    cfd = DI("cf", [128, NCF])
    sp_prev = DI("sp_prev", [240, 1024])
    sret = DI("sret", [16, 4, 128, 1024])
    wpool_d = DI("wpool", [128, 2048])
    w_gu = DI("w_gu", [2, 11, 128, 4096])
    w_d0 = DI("w_d0", [2, 4, 128, 3072])
    w_d1 = DI("w_d1", [2, 4, 128, 2560])
    w_in = DI("w_in", [4, 3, 128, 4096])
    w_out = DI("w_out", [4, 128, 4096])
    yT = DO("yT", [1024, 2 * NT])
    npp = DO("npp", [1024, 16])
    nps_x = DO("nps_x", [1024, 16])
    nps_prev = DO("nps_prev", [16, 14, 1024])
    nrp = DO("nrp", [4, 2, 128, 512])
    nrs = DO("nrs", [16, 4, 128, 1024])
    sscr = nc.dram_tensor("sscr", [4, 2, 128, 512], F32, kind="Internal").ap()
    if dbg:
        dbg_h = DO("dbg_h", [4, 2, 1024, NT])

    lg = _gammas()
    gam = [float(np.exp(lg[h])) for h in range(4)]
    gam128 = [float(np.exp(np.float32(128.0) * lg[h])) for h in range(4)]

    st = contextlib.ExitStack()
    with st:
        P = Prog(nc, st)

        def SB(name, shape, dt):
            return st.enter_context(nc.sbuf_tensor(name, list(shape), dt))

        def PS(name, shape, dt):
            return st.enter_context(nc.psum_tensor(name, list(shape), dt))

        hT = SB("hT", [128, 8, NT], F32)
        xn = SB("xn", [128, 8, NT], BF16)
        RA = SB("RA", [128, 2112], F32)
        RB = SB("RB", [128, 3264], F32)
        RC = SB("RC", [128, 2112], F32)
        cs = SB("cs_sb", [128, 2, NT], F32)
        cf = SB("cf_sb", [128, NCF], F32)
        w32 = SB("w32", [128, 40], F32)
        identb = SB("identb", [128, 128], BF16)
        onesb = SB("onesb", [128, 128], BF16)
        epsb = SB("epsb", [128, 2], F32)
        rstd = SB("rstd", [128, NT], F32)
        S32w = SB("S32w", [128, 2, 512], F32)
        SbfR = [SB("Sbf%d" % i, [128, 2, 512], BF16) for i in range(3)]
        wslots = [SB("wslot%d" % i, [128, 4096], BF16) for i in range(4)]
        sring = Ring('sr', [SB("sring%d" % i, [128, 2, 512], F32) for i in range(4)])
        rotS = [[SB("rot%d_%d" % (j, i), [128, 352], F32) for i in range(4)] for j in range(2)]
        sgR = Ring('sg', [SB("sg%d" % i, [128, 512], BF16) for i in range(2)])
        pmA = SB("pmA", [128, 9, 128], BF16)
        qcA = SB("qcA", [128, 9, 2, 128], BF16)
        kdA = SB("kdA", [128, 9, 256], BF16)
        kdS = SB("kdS", [128, 256], BF16)
        idw = SB("idw", [128, 8, 128], BF16)
        P2 = SB("P2", [128, 2, 32], F32)
        Q2 = SB("Q2", [128, 2, 32], F32)
        vS = SB("vS", [128, 512], BF16)
        oacc = SB("oacc", [128, 512], F32)
        RC2 = SB("RC2", [128, 2080], F32)
        onR = Ring('on', [SB("on%d" % i, [128, 512], BF16) for i in range(3)])
        affR = Ring('aff', [SB("aff%d" % i, [128, 4, 128], F32) for i in range(1)])
        stR = Ring('st', [SB("stt%d" % i, [128, 12], F32) for i in range(2)])
        halo = SB("halo", [128, 8, 16], F32)
        xs = SB("xs", [128, 8, 16], F32)
        prevsum = SB("prevsum", [128, 8, 16], F32)
        kmaskR = Ring('kmask', [SB("kmask%d" % i, [128, 4, 256], BF16) for i in range(1)])
        snbR = [SB("snb%d" % i, [128, 2, 512], BF16) for i in range(2)]
        qm = SB("qm", [128, 2, 16, 16], BF16)
        qsf = SB("qsf", [128, 2, 16], F32)

        pbR = Ring('pb', [PS("pb%d" % i, [128, 512], F32) for i in range(6)])
        psU = PS("psU", [128, 2, 512], F32)

        RA_bf = RA[:, 0:2080].bitcast(BF16).rearrange("p (k n) -> p k n", k=4)
        RA_x = RA[:, 0:2112].rearrange("p (k n) -> p k n", k=2)
        RB_hid = RB[:, 0:3120].bitcast(BF16).rearrange("p (k n) -> p k n", k=6)
        RB_v = RB[:, 0:2304].bitcast(BF16).rearrange("p (k n) -> p k n", k=9)
        RB_diff = RB[:, 0:1040].bitcast(BF16).rearrange("p (k n) -> p k n", k=2)
        RB_q = RB[:, 1040:3152].rearrange("p (k n) -> p k n", k=2)
        RC_bf = RC[:, 0:2080].bitcast(BF16).rearrange("p (k n) -> p k n", k=4)
        RC_p = RC[:, 0:2112].rearrange("p (k n) -> p k n", k=2)
        RC_prev = RC[:, 0:2048].rearrange("p (k n) -> p k n", k=2)
        RC2_bf = RC2[:, 0:2080].bitcast(BF16).rearrange("p (k n) -> p k n", k=4)
        GB = [(RC_bf, 'RC'), (RC2_bf, 'RC2')]

        def hid(i):
            if i < 4:
                return RA_bf[:, i, :], 'RA'
            if i < 10:
                return RB_hid[:, i - 4, :], 'RB'
            return RC_bf[:, i - 10, :], 'RC'

        def hk(c):
            return [('h', c, t) for t in range(3)]

        wl = []
        for half in range(2):
            wl.append(('pool', wpool_d, lambda s: s[:, 0:2048]))
            for l in range(2):
                if l == 1:
                    def w_in_t(h, j):
                        return ('in', w_in[h, j].rearrange("p (k n) -> p k n", k=8),
                                lambda s: s[:, 0:4096].rearrange("p (k n) -> p k n", k=8))

                    def w_out_t(h):
                        return ('out', w_out[h].rearrange("p (k n) -> p k n", k=4),
                                lambda s: s[:, 0:4096].rearrange("p (k n) -> p k n", k=4))
                    for h in range(4):
                        for j in range(3):
                            wl.append(w_in_t(h, j))
                        if half == 0:
                            wl.append(w_out_t(h))
                        elif h > 0:
                            wl.append(w_out_t(h - 1))
                    if half == 1:
                        wl.append(w_out_t(3))
                for fh in range(2):
                    ntile = 6 if fh == 0 else 5
                    for j in range(ntile):
                        jj = j if fh == 0 else 6 + j
                        wl.append(('gu', w_gu[l, jj].rearrange("p (k n) -> p k n", k=8),
                                   lambda s: s[:, 0:4096].rearrange("p (k n) -> p k n", k=8)))
                    nk = 12 if fh == 0 else 10
                    wd = w_d0 if fh == 0 else w_d1
                    for mt in range(4):
                        wl.append(('d', wd[l, mt].rearrange("p (k n) -> p k n", k=nk),
                                   lambda s, nk=nk: s[:, 0:nk * 256].rearrange("p (k n) -> p k n", k=nk)))
        wstate = {'issued': 0, 'cur': 0}
        PREF = 3

        def wget(kind):
            i = wstate['cur']
            assert wl[i][0] == kind, (i, wl[i][0], kind)
            while wstate['issued'] < min(len(wl), i + PREF + 1):
                j = wstate['issued']
                s = j % 4
                view = wl[j][2](wslots[s])
                P.dma('pool', [(view, wl[j][1])], writes=[('w', s)])
                wstate['issued'] += 1
            wstate['cur'] += 1
            s = i % 4
            return wl[i][2](wslots[s]), ('w', s)

        P.dma('sp', [(cf[:], cfd[:, :])], writes=['cf'])
        P.op('act', lambda h: h.mul(out=w32[:], in_=cf[:, 0:40], mul=32.0), reads=['cf'], writes=['w32'])
        P.op('dve', lambda h: h.tensor_copy(out=identb[:], in_=cf[:, C_ID:C_ID + 128]), reads=['cf'], writes=['identb'])
        P.op('pool', lambda h: h.memset(onesb[:], 1.0), writes=['onesb'])
        for g_, w_ in enumerate(WINS):
            P.op('act', lambda h, g_=g_, w_=w_: h.mul(out=idw[:, 2 * g_, :], in_=identb[:], mul=1.0 / w_ - 1.0),
                 reads=['identb'], writes=[('idw', 2 * g_)])
            P.op('act', lambda h, g_=g_, w_=w_: h.mul(out=idw[:, 2 * g_ + 1, :], in_=identb[:], mul=1.0 / w_),
                 reads=['identb'], writes=[('idw', 2 * g_ + 1)])
        P.op('pool', lambda h: h.memset(epsb[:, 0:1], 1024.0 * EPS), writes=['epsb'])
        P.op('pool', lambda h: h.memset(epsb[:, 1:2], EPS), writes=['epsb'])

        def mm(out, lhsT, rhs, start, stop, reads, writes):
            return P.op('pe', lambda h: h.matmul(out, lhsT=lhsT, rhs=rhs, start=start, stop=stop), reads=reads, writes=writes)

        def tr(out, in_, ident, reads, writes):
            return P.op('pe', lambda h: h.transpose(out, in_, ident), reads=reads, writes=writes)

        def norm(wcol, kind):
            for c in range(8):
                if c in (2, 5):
                    P.op('pool', lambda h, c=c: h.tensor_tensor(out=xn[:, c, :], in0=hT[:, c, :], in1=hT[:, c, :], op=ALU.mult),
                         reads=hk(c), writes=[('xn', c)])
                else:
                    P.op('act', lambda h, c=c: h.activation(out=xn[:, c, :], in_=hT[:, c, :], func=ACT.Square),
                         reads=hk(c), writes=[('xn', c)])
            for ti, (t0, tn) in enumerate(TBS):
                pb, pk = pbR.next()
                for c in range(8):
                    mm(pb[:, 0:tn], onesb[:], xn[:, c, t0:t0 + tn], c == 0, c == 7,
                       ['onesb', ('xn', c)], [pk])
                P.op('act', lambda h, pb=pb, t0=t0, tn=tn: h.activation(
                    out=rstd[:, t0:t0 + tn], in_=pb[:, 0:tn], func=ACT.Sqrt, bias=epsb[:, 0:1], scale=1.0),
                    reads=[pk, 'epsb'], writes=[('rstd', ti)])
                P.op('dve', lambda h, t0=t0, tn=tn: h.reciprocal(out=rstd[:, t0:t0 + tn], in_=rstd[:, t0:t0 + tn]),
                     reads=[('rstd', ti)], writes=[('rstd', ti)])
            rk = [('rstd', t) for t in range(3)]
            if kind == 'bf16':
                for c in range(8):
                    P.op('dve', lambda h, c=c: h.scalar_tensor_tensor(
                        out=xn[:, c, :], in0=hT[:, c, :], scalar=w32[:, wcol + c:wcol + c + 1], in1=rstd[:],
                        op0=ALU.mult, op1=ALU.mult), reads=hk(c) + rk + ['w32'], writes=[('xn', c)])
            elif kind == 'final':
                for c in range(8):
                    P.op('dve', lambda h, c=c: h.scalar_tensor_tensor(
                        out=hT[:, c, :], in0=hT[:, c, :], scalar=w32[:, wcol + c:wcol + c + 1], in1=rstd[:],
                        op0=ALU.mult, op1=ALU.mult), reads=hk(c) + rk + ['w32'], writes=hk(c))

        def hadd(m, ti, t0, tn, pb, pk, scale_ap=None):
            if scale_ap is None:
                P.op('dve', lambda h: h.tensor_tensor(out=hT[:, m, t0:t0 + tn], in0=pb[:, 0:tn],
                                                       in1=hT[:, m, t0:t0 + tn], op=ALU.add),
                     reads=[pk, ('h', m, ti)], writes=[('h', m, ti)])
            else:
                P.op('dve', lambda h: h.scalar_tensor_tensor(out=hT[:, m, t0:t0 + tn], in0=pb[:, 0:tn],
                                                              scalar=scale_ap, in1=hT[:, m, t0:t0 + tn],
                                                              op0=ALU.mult, op1=ALU.add),
                     reads=[pk, ('h', m, ti), 'cf'], writes=[('h', m, ti)])

        def dump(idx, half):
            if not dbg:
                return
            P.dma('sp', [(dbg_h[idx, half].rearrange("(k p) n -> p k n", p=128), hT[:])],
                  reads=[k for c in range(8) for k in hk(c)], semkey=('dbg', idx, half), is_out=True)

        def pool_layer(half):
            norm(C_NM0, 'pool')
            wv, wk = wget('pool')
            wp = wv.rearrange("p (g k n) -> p g k n", g=4, k=2)
            rk = [('rstd', t) for t in range(3)]
            if half == 1:
                for c in range(8):
                    P.op('dve', lambda h, c=c: h.scalar_tensor_tensor(
                        out=xs[:, c, :], in0=hT[:, c, 1024:1040], scalar=w32[:, C_NM0 + c:C_NM0 + c + 1],
                        in1=rstd[:, 1024:1040], op0=ALU.mult, op1=ALU.mult),
                        reads=[('h', c, 2), ('rstd', 2), 'w32'], writes=[('xs', c)])
                P.dma('sp', [(nps_x.rearrange("(k p) n -> p k n", p=128), xs[:])],
                      reads=[('xs', c) for c in range(8)], semkey='o_npsx', is_out=True)
                P.dma('sp', [(nps_prev[:, :, :], sp_prev.rearrange("(b j) d -> b j d", j=15)[:, 1:15, :])],
                      semkey='o_npsprev', is_out=True)
                for c in range(8):
                    g = c // 2
                    pb, pk = pbR.next()
                    for kc, kn in ((0, 128), (1, 112)):
                        off = C_AW + (kc * 4 + g) * 16
                        mm(pb[:, 0:16], RC_prev[0:kn, kc, c * 128:(c + 1) * 128], cf[0:kn, off:off + 16],
                           kc == 0, kc == 1, ['RC', 'cf'], [pk])
                    P.op('act', lambda h, c=c, pb=pb: h.copy(out=prevsum[:, c, :], in_=pb[:, 0:16]),
                         reads=[pk], writes=[('prevsum', c)])
            XB = [(RA_x, 'RA'), (RB_q, 'RB')]
            XbB = [(RC_bf[:, 0:2, :], ('RCx', 0)), (RC_bf[:, 2:4, :], ('RCx', 1))]

            def stageX(g):
                X, xkey = XB[g % 2]
                P.op('pool', lambda h: h.memset(X[:, :, 0:16], 0.0), writes=[xkey])
                for j in range(2):
                    c = 2 * g + j
                    P.op('dve', lambda h, c=c, j=j: h.scalar_tensor_tensor(
                        out=X[:, j, 32:1056], in0=hT[:, c, 0:1024], scalar=w32[:, C_NM0 + c:C_NM0 + c + 1],
                        in1=rstd[:, 0:1024], op0=ALU.mult, op1=ALU.mult),
                        reads=hk(c) + rk + ['w32'], writes=[xkey])
                    if half == 0:
                        P.op('dve', lambda h, c=c, j=j: h.scalar_tensor_tensor(
                            out=X[:, j, 16:32], in0=hT[:, c, 1024:1040], scalar=w32[:, C_NM0 + c:C_NM0 + c + 1],
                            in1=rstd[:, 1024:1040], op0=ALU.mult, op1=ALU.mult),
                            reads=hk(c) + rk + ['w32'], writes=[xkey])
                    else:
                        P.op('dve', lambda h, c=c, j=j: h.tensor_copy(out=X[:, j, 16:32], in_=halo[:, c, :]),
                             reads=[('halo', c)], writes=[xkey])

            stageX(0)
            for g in range(4):
                w = WINS[g]
                X, xkey = XB[g % 2]
                Xb, xbkey = XbB[g % 2]
                P.op('act', lambda h, X=X, Xb=Xb: h.copy(out=Xb, in_=X[:, :, 16:1056]), reads=[xkey],
                     writes=[xbkey] + (['RC'] if g < 2 else []))
                if g < 3:
                    stageX(g + 1)
                for j in range(2):
                    for (t0, tn) in ((0, 342), (342, 341), (683, 341)):
                        pb, pk = pbR.next()
                        for k in range(w):
                            mm(pb[:, 0:tn], idw[:, 2 * g + (1 if k else 0), :],
                               Xb[:, j, 16 + t0 - k:16 + t0 - k + tn], k == 0, k == w - 1,
                               [('idw', 2 * g + (1 if k else 0)), xbkey], [pk])
                        P.op('act', lambda h, pb=pb, j=j, t0=t0, tn=tn: h.copy(out=RB_diff[:, j, t0:t0 + tn], in_=pb[:, 0:tn]),
                             reads=[pk], writes=['RBd'])
                src, skey = X, xkey
                if half == 0:
                    bufs = [(P2, 'P2'), (Q2, 'Q2')]
                    sh = 1
                    bi = 0
                    while sh < w:
                        dst, dkey = bufs[bi % 2]
                        lo = 2 * sh - 1
                        P.op('dve', lambda h, dst=dst, src=src, sh=sh, lo=lo: h.tensor_tensor(
                            out=dst[:, :, lo:32], in0=src[:, :, lo:32], in1=src[:, :, lo - sh:32 - sh], op=ALU.add),
                            reads=[skey], writes=[dkey])
                        src, skey = dst, dkey
                        sh *= 2
                        bi += 1
                if half == 0:
                    tt, tk = affR.next()
                    tv = tt[:, 0, 0:32].rearrange("p (j n) -> p j n", j=2)
                    ic = cf[:, C_INVC + g * 16:C_INVC + (g + 1) * 16].unsqueeze(1).to_broadcast([128, 2, 16])
                    P.op('dve', lambda h, src=src, tv=tv, ic=ic: h.tensor_tensor(
                        out=tv, in0=src[:, :, 16:32], in1=ic, op=ALU.mult), reads=[skey, 'cf'], writes=[tk])
                    P.op('dve', lambda h, tv=tv, X=X: h.tensor_tensor(
                        out=RB_diff[:, :, 1024:1040], in0=tv, in1=X[:, :, 16:32], op=ALU.subtract),
                        reads=[tk, xkey], writes=['RBd'])
                    for j in range(2):
                        c = 2 * g + j
                        P.op('act', lambda h, c=c, j=j, X=X: h.copy(out=halo[:, c, :], in_=X[:, j, 1040:1056]),
                             reads=[xkey], writes=[('halo', c)])
                else:
                    for j in range(2):
                        c = 2 * g + j
                        tt, tk = affR.next()
                        P.op('dve', lambda h, c=c, tt=tt: h.tensor_tensor(
                            out=tt[:, 0, 0:16], in0=prevsum[:, c, :], in1=xs[:, c, :], op=ALU.add),
                            reads=[('prevsum', c), ('xs', c)], writes=[tk])
                        P.op('dve', lambda h, c=c, j=j, tt=tt, w=w: h.scalar_tensor_tensor(
                            out=RB_diff[:, j, 1024:1040], in0=tt[:, 0, 0:16], scalar=1.0 / w, in1=xs[:, c, :],
                            op0=ALU.mult, op1=ALU.subtract), reads=[tk, ('xs', c)], writes=['RBd'])
                        P.dma('sp', [(npp[c * 128:(c + 1) * 128, :], X[:, j, 1040:1056])], reads=[xkey],
                              semkey=('o_npp', c), is_out=True)
                for m in range(2):
                    cm = 2 * g + m
                    for ti, (t0, tn) in enumerate(TBS):
                        pb, pk = pbR.next()
                        for kc in range(2):
                            mm(pb[:, 0:tn], wp[:, g, kc, m * 128:(m + 1) * 128], RB_diff[:, kc, t0:t0 + tn],
                               kc == 0, kc == 1, [wk, 'RBd'], [pk])
                        hadd(cm, ti, t0, tn, pb, pk, scale_ap=cf[:, C_PS + cm:C_PS + cm + 1])

        def ffn(l):
            norm(C_NF0 if l == 0 else C_NF1, 'bf16')
            for fh in range(2):
                ntile = 6 if fh == 0 else 5
                nk = 2 * ntile
                for j in range(ntile):
                    wv, wk = wget('gu')
                    for fc in range(2):
                        hv, hkey = hid(2 * j + fc)
                        for ti, (t0, tn) in enumerate(TBS):
                            pg, pgk = pbR.next()
                            pu, puk = pbR.next()
                            for k in range(8):
                                mm(pg[:, 0:tn], wv[:, k, fc * 128:(fc + 1) * 128], xn[:, k, t0:t0 + tn],
                                   k == 0, k == 7, [wk, ('xn', k)], [pgk])
                            for k in range(8):
                                mm(pu[:, 0:tn], wv[:, k, 256 + fc * 128:256 + (fc + 1) * 128], xn[:, k, t0:t0 + tn],
                                   k == 0, k == 7, [wk, ('xn', k)], [puk])
                            sg, sgk = sgR.next()
                            P.op('act', lambda h, sg=sg, pg=pg, tn=tn: h.activation(out=sg[:, 0:tn], in_=pg[:, 0:tn], func=ACT.Silu),
                                 reads=[pgk], writes=[sgk])
                            P.op('dve', lambda h, hv=hv, pu=pu, sg=sg, t0=t0, tn=tn: h.tensor_tensor(
                                out=hv[:, t0:t0 + tn], in0=pu[:, 0:tn], in1=sg[:, 0:tn], op=ALU.mult),
                                reads=[puk, sgk], writes=[hkey])
                for mt in range(4):
                    wv, wk = wget('d')
                    for mmi in range(2):
                        m = 2 * mt + mmi
                        for ti, (t0, tn) in enumerate(TBS):
                            pb, pk = pbR.next()
                            for kc in range(nk):
                                hv, hkey = hid(kc)
                                mm(pb[:, 0:tn], wv[:, kc, mmi * 128:(mmi + 1) * 128], hv[:, t0:t0 + tn],
                                   kc == 0, kc == nk - 1, [wk, hkey], [pk])
                            hadd(m, ti, t0, tn, pb, pk)

        def gn_norm(po, pok, n):
            return gn_norm_b(*gn_norm_a(po, pok, n))

        def gn_norm_a(po, pok, n):
            stt, stk = stR.next()
            P.op('dve', lambda hh: hh.bn_stats(out=stt[0:n, 0:6], in_=po[0:n, :]), reads=[pok], writes=[stk])
            P.op('dve', lambda hh: hh.bn_aggr(out=stt[0:n, 6:8], in_=stt[0:n, 0:6]), reads=[stk], writes=[stk])
            P.op('act', lambda hh: hh.activation(out=stt[0:n, 8:9], in_=stt[0:n, 7:8], func=ACT.Sqrt, bias=epsb[0:n, 1:2], scale=1.0),
                 reads=[stk, 'epsb'], writes=[stk])
            return po, pok, n, stt, stk

        def gn_norm_b(po, pok, n, stt, stk):
            P.op('dve', lambda hh: hh.reciprocal(out=stt[0:n, 8:9], in_=stt[0:n, 8:9]), reads=[stk], writes=[stk])
            on, onk = onR.next()
            P.op('dve', lambda hh: hh.scalar_tensor_tensor(out=stt[0:n, 9:10], in0=stt[0:n, 6:7], scalar=-1.0,
                                                           in1=stt[0:n, 8:9], op0=ALU.mult, op1=ALU.mult),
                 reads=[stk], writes=[stk])
            P.op('act', lambda hh: hh.activation(out=on[0:n, :], in_=po[0:n, :], func=ACT.Identity,
                                                 bias=stt[0:n, 9:10], scale=stt[0:n, 8:9]),
                 reads=[pok, stk], writes=[onk])
            return on, onk

        def gn_gate(on, onk, n, c0, h, gsel=0):
            gbuf, gkey = GB[gsel]
            pg, pgk = pbR.next()
            pv = pg[:].bitcast(BF16)[:, 0:512].rearrange("p (e n) -> p e n", e=4)
            for e4 in range(4):
                tr(pv[:, e4, 0:n], on[0:n, e4 * 128:(e4 + 1) * 128], identb[0:n, 0:n], [onk, 'identb'], [pgk])
            af, afk = affR.next()
            gw = cf[:, C_GNW + h * 4:C_GNW + h * 4 + 4].unsqueeze(2).to_broadcast([128, 4, n])
            gb = cf[:, C_GNB + h * 4:C_GNB + h * 4 + 4].unsqueeze(2).to_broadcast([128, 4, n])
            P.op('dve', lambda hh: hh.tensor_tensor(out=af[:, :, 0:n], in0=pv[:, :, 0:n], in1=gw, op=ALU.mult),
                 reads=[pgk, 'cf'], writes=[afk])
            P.op('pool', lambda hh: hh.tensor_tensor(out=af[:, :, 0:n], in0=af[:, :, 0:n], in1=gb, op=ALU.add),
                 reads=[afk, 'cf'], writes=[afk])
            P.op('pool', lambda hh: hh.tensor_tensor(out=gbuf[:, :, c0:c0 + n], in0=af[:, :, 0:n],
                                                     in1=gbuf[:, :, c0:c0 + n], op=ALU.mult),
                 reads=[afk, gkey], writes=[(gkey + 'g', c0)])

        def chunk_prep(h, idx, n, c0, first):
            ps, psk = pbR.next()
            for c in range(2):
                mm(ps[0:n, 0:n], RA_bf[:, 2 + c, c0:c0 + n], RA_bf[:, c, c0:c0 + n], c == 0, c == 1, ['RA'], [psk])
            P.op('dve', lambda hh: hh.tensor_tensor(out=pmA[0:n, idx, 0:n], in0=ps[0:n, 0:n],
                                                    in1=cf[0:n, C_MASK + h * 128:C_MASK + h * 128 + n], op=ALU.mult),
                 reads=[psk, 'cf'], writes=[('pmA', idx)])
            pt, ptk = pbR.next()
            psT = pt[:].bitcast(BF16)[:, 0:256]
            ptv = psT.rearrange("p (c n) -> p c n", c=2)
            for c in range(2):
                tr(ptv[0:n, c, :], RA_bf[:, 2 + c, c0:c0 + n], identb[:, :], ['RA', 'identb'], [ptk])
            kcol = (C_KDEC if n == 128 else C_KDECM) + h
            P.op('act', lambda hh: hh.activation(out=kdA[0:n, idx, :], in_=psT[0:n, :], func=ACT.Copy,
                                                 scale=cf[0:n, kcol:kcol + 1]),
                 reads=[ptk, 'cf'], writes=[('kdA', idx)])
            if not first:
                cr = cf[:, C_CROSS + h * 128:C_CROSS + h * 128 + n].unsqueeze(1).to_broadcast([128, 2, n])
                P.op('pool', lambda hh: hh.tensor_tensor(out=qcA[:, idx, :, 0:n], in0=RA_bf[:, 0:2, c0:c0 + n], in1=cr, op=ALU.mult),
                     reads=['RA', 'cf'], writes=[('qcA', idx)])

        def chunk_state(h, idx, t, n, first, sidx):
            for c in range(2):
                mm(psU[:, c, :], kdA[0:n, idx, c * 128:(c + 1) * 128], RB_v[0:n, t, :], True, True,
                   [('kdA', idx), 'RB'], ['psU'])
            if first:
                P.op('dve', lambda hh: hh.tensor_copy(out=S32w[:], in_=psU[:]), reads=['psU'], writes=['S32w'])
            else:
                P.op('dve', lambda hh: hh.scalar_tensor_tensor(out=S32w[:], in0=S32w[:], scalar=gam128[h], in1=psU[:],
                                                               op0=ALU.mult, op1=ALU.add),
                     reads=['psU', 'S32w'], writes=['S32w'])
            scur = SbfR[sidx % 3]
            P.op('act', lambda hh: hh.copy(out=scur[:], in_=S32w[:]), reads=['S32w'], writes=[('Sbf', sidx % 3)])

        def chunk_intra(idx, t, n, first):
            po, pok = pbR.next()
            mm(po[0:n, :], pmA[0:n, idx, 0:n], RB_v[0:n, t, :], True, first, [('pmA', idx), 'RB'], [pok])
            return po, pok

        def chunk_cross(idx, n, first, sidx, po, pok):
            if not first:
                sprev = SbfR[(sidx - 1) % 3]
                for c in range(2):
                    mm(po[0:n, :], qcA[:, idx, c, 0:n], sprev[:, c, :], False, c == 1,
                       [('qcA', idx), ('Sbf', (sidx - 1) % 3)], [pok])
            return gn_norm_a(po, pok, n)

        def sample_prep(h):
            pt, ptk = pbR.next()
            psT = pt[:].bitcast(BF16)[:, 0:256]
            ptv = psT.rearrange("p (c n) -> p c n", c=2)
            for c in range(2):
                tr(ptv[0:16, c, :], RA_bf[:, 2 + c, 1024:1040], identb[:, :], ['RA', 'identb'], [ptk])
            P.op('act', lambda hh: hh.copy(out=kdS[0:16, :], in_=psT[0:16, :]), reads=[ptk], writes=['kdS'])
            P.op('act', lambda hh: hh.copy(out=vS[0:16, :], in_=RB_v[0:16, 8, :]), reads=['RB'], writes=['vS'])
            P.op('act', lambda hh: hh.copy(out=qsf[:], in_=RA_bf[:, 0:2, 1024:1040]), reads=['RA'], writes=['qsf'])
            d16 = cf[:, C_D16:C_D16 + 256].rearrange("p (a b) -> p a b", a=16)
            for c in range(2):
                P.op('dve', lambda hh, c=c: hh.tensor_tensor(out=qm[:, c, :, :],
                                                             in0=qsf[:, c, :].unsqueeze(2).to_broadcast([128, 16, 16]),
                                                             in1=d16, op=ALU.mult), reads=['qsf', 'cf'], writes=['qm'])

        kmcur = [None, None]

        def sample_unit(h, b):
            if b % 4 == 0:
                km, kmk = kmaskR.next()
                kmcur[0], kmcur[1] = km, kmk
                i16 = cf[0:16, C_I16 + b:C_I16 + b + 4].unsqueeze(2).to_broadcast([16, 4, 256])
                P.op('dve', lambda hh, km=km, i16=i16: hh.tensor_tensor(
                    out=km[0:16, :, :], in0=kdS[0:16, :].unsqueeze(1).to_broadcast([16, 4, 256]), in1=i16, op=ALU.mult),
                    reads=['kdS', 'cf'], writes=[kmk])
            km, kmk = kmcur

            def s_load(bb):
                u = h * 16 + bb
                t_, k_ = sring.tiles[u % 4], ('sr', u % 4)
                P.dma('sp', [(t_[:], sret[bb, h].rearrange("p (c e) -> p c e", c=2))], writes=[k_])
            if b == 0:
                s_load(0)
                s_load(1)
            if b + 2 < 16:
                s_load(b + 2)
            u = h * 16 + b
            sl, slk = sring.tiles[u % 4], ('sr', u % 4)
            for c in range(2):
                mm(psU[:, c, :], km[0:16, b % 4, c * 128:(c + 1) * 128], vS[0:16, :], True, True,
                   [kmk, 'vS'], ['psU'])
            P.op('dve', lambda hh, sl=sl: hh.scalar_tensor_tensor(out=sl[:], in0=sl[:], scalar=gam[h], in1=psU[:],
                                                                  op0=ALU.mult, op1=ALU.add),
                 reads=['psU', slk], writes=[slk])
            P.dma('sp', [(nrs[b, h].rearrange("p (c e) -> p c e", c=2), sl[:])], reads=[slk], is_out=True)
            if spend:
                sample_q(*spend.pop(0))
            spend.append((b, sl, slk))

        spend = []
        sqpend = []

        def sample_q(b, sl, slk):
            sb_, sbk = snbR[b % 2], ('snb', b % 2)
            if sqpend:
                sample_o(*sqpend.pop(0))
            P.op('act', lambda hh, sl=sl, sb_=sb_: hh.copy(out=sb_[:], in_=sl[:]), reads=[slk], writes=[sbk])
            sqpend.append((b, sb_, sbk))

        def sample_o(b, sb_, sbk):
            po, pok = pbR.next()
            for c in range(2):
                mm(po[0:16, :], qm[:, c, b, :], sb_[:, c, :], c == 0, c == 1, ['qm', sbk], [pok])
            if b == 0:
                P.op('dve', lambda hh, po=po: hh.tensor_copy(out=oacc[0:16, :], in_=po[0:16, :]), reads=[pok], writes=['oacc'])
            else:
                P.op('dve', lambda hh, po=po: hh.tensor_tensor(out=oacc[0:16, :], in0=po[0:16, :], in1=oacc[0:16, :], op=ALU.add),
                     reads=[pok, 'oacc'], writes=['oacc'])

        def sample_finish(h, gsel):
            while spend:
                sample_q(*spend.pop(0))
            while sqpend:
                sample_o(*sqpend.pop(0))
            on, onk = gn_norm(oacc, 'oacc', 16)
            gn_gate(on, onk, 16, 1024, h, gsel)

        def ret_layer(half):
            norm(C_NM1, 'bf16')
            step = [0]

            def proj(h, gsel, tick):
                gbuf, gkey = GB[gsel]
                wv, wk = wget('in')
                for qk in range(2):
                    for ti, (t0, tn) in enumerate(TBS):
                        rs = step[0] % 2
                        step[0] += 1
                        rot = rotS[rs]
                        rkk = ['rot%d_%d' % (rs, i) for i in range(4)]
                        pa, pak = pbR.next()
                        pbb, pbk = pbR.next()
                        for k in range(8):
                            mm(pa[:, 0:tn], wv[:, k, (2 * qk) * 128:(2 * qk + 1) * 128], xn[:, k, t0:t0 + tn],
                               k == 0, k == 7, [wk, ('xn', k)], [pak])
                        for k in range(8):
                            mm(pbb[:, 0:tn], wv[:, k, (2 * qk + 1) * 128:(2 * qk + 2) * 128], xn[:, k, t0:t0 + tn],
                               k == 0, k == 7, [wk, ('xn', k)], [pbk])
                        co = cs[:, 0, t0:t0 + tn]
                        si = cs[:, 1, t0:t0 + tn]
                        P.op('dve', lambda hh, pa=pa, co=co, tn=tn, rot=rot: hh.tensor_tensor(out=rot[0][:, 0:tn], in0=pa[:, 0:tn], in1=co, op=ALU.mult),
                             reads=[pak, 'cs'], writes=[rkk[0]])
                        P.op('dve', lambda hh, pbb=pbb, si=si, tn=tn, rot=rot: hh.tensor_tensor(out=rot[1][:, 0:tn], in0=pbb[:, 0:tn], in1=si, op=ALU.mult),
                             reads=[pbk, 'cs'], writes=[rkk[1]])
                        P.op('dve', lambda hh, pa=pa, si=si, tn=tn, rot=rot: hh.tensor_tensor(out=rot[2][:, 0:tn], in0=pa[:, 0:tn], in1=si, op=ALU.mult),
                             reads=[pak, 'cs'], writes=[rkk[2]])
                        P.op('dve', lambda hh, pbb=pbb, co=co, tn=tn, rot=rot: hh.tensor_tensor(out=rot[3][:, 0:tn], in0=pbb[:, 0:tn], in1=co, op=ALU.mult),
                             reads=[pbk, 'cs'], writes=[rkk[3]])
                        P.op('pool', lambda hh, qk=qk, t0=t0, tn=tn, rot=rot: hh.tensor_tensor(out=RA_bf[:, 2 * qk, t0:t0 + tn], in0=rot[0][:, 0:tn],
                                                                                    in1=rot[1][:, 0:tn], op=ALU.subtract),
                             reads=[rkk[0], rkk[1]], writes=['RA'])
                        P.op('pool', lambda hh, qk=qk, t0=t0, tn=tn, rot=rot: hh.tensor_tensor(out=RA_bf[:, 2 * qk + 1, t0:t0 + tn], in0=rot[2][:, 0:tn],
                                                                                    in1=rot[3][:, 0:tn], op=ALU.add),
                             reads=[rkk[2], rkk[3]], writes=['RA'])
                        tick()
                wv, wk = wget('in')
                for t in range(9):
                    n = 128 if t < 8 else 16
                    c0 = t * 128
                    pb, pk = pbR.next()
                    for k in range(8):
                        mm(pb[0:n, :], xn[:, k, c0:c0 + n], wv[:, k, :], k == 0, k == 7, [wk, ('xn', k)], [pk])
                    P.op('act', lambda hh, pb=pb, t=t, n=n: hh.copy(out=RB_v[0:n, t, :], in_=pb[0:n, :]),
                         reads=[pk], writes=['RB'])
                    tick()
                wv, wk = wget('in')
                for m in range(4):
                    for ti, (t0, tn) in enumerate(TBS):
                        pb, pk = pbR.next()
                        for k in range(8):
                            mm(pb[:, 0:tn], wv[:, k, m * 128:(m + 1) * 128], xn[:, k, t0:t0 + tn],
                               k == 0, k == 7, [wk, ('xn', k)], [pk])
                        P.op('act', lambda hh, pb=pb, m=m, t0=t0, tn=tn: hh.activation(out=gbuf[:, m, t0:t0 + tn], in_=pb[:, 0:tn], func=ACT.Silu),
                             reads=[pk], writes=[gkey])
                        tick()

            def chunks(h, gsel, tiles):
                if half == 1:
                    P.dma('sp', [(S32w[:], sscr[h].rearrange("c p e -> p c e"))], reads=[('nrpd', h)], writes=['S32w'],
                          semkey=('nrp_ld', h))
                    P.op('act', lambda hh: hh.copy(out=SbfR[0][:], in_=S32w[:]), reads=['S32w'], writes=[('Sbf', 0)])
                for idx, (t, n, c0) in enumerate(tiles):
                    chunk_prep(h, idx, n, c0, half == 0 and idx == 0)
                pend = []
                cpend = []
                for idx, (t, n, c0) in enumerate(tiles):
                    sidx = idx if half == 0 else idx + 1
                    first = (half == 0 and idx == 0)
                    po, pok = chunk_intra(idx, t, n, first)
                    st_ = chunk_cross(*cpend.pop(0)) if cpend else None
                    chunk_state(h, idx, t, n, first, sidx)
                    if st_ is not None:
                        on, onk = gn_norm_b(*st_)
                        pend.append((on, onk) + cmeta.pop(0))
                    else:
                        cmeta = []
                    cpend.append((idx, n, first, sidx, po, pok))
                    cmeta.append((n, c0))
                    if len(pend) > 1:
                        gn_gate(*pend.pop(0), h, gsel)
                while cpend:
                    on, onk = gn_norm_b(*chunk_cross(*cpend.pop(0)))
                    pend.append((on, onk) + cmeta.pop(0))
                while pend:
                    gn_gate(*pend.pop(0), h, gsel)
                P.dma('sp', [((sscr if half == 0 else nrp)[h].rearrange("c p e -> p c e"), S32w[:])], reads=['S32w'],
                      writes=[('nrpd', h)], semkey=('o_nrp', h), is_out=True)

            def outproj(h, gsel, tiles, tick=None, before_special=None):
                gbuf, gkey = GB[gsel]
                wv, wk = wget('out')
                allc = [c0 for (_, _, c0) in tiles] + ([1024] if half == 1 else [])
                for ti, (t0, tn) in enumerate(TBS):
                    if ti == 2 and before_special is not None:
                        before_special()
                    gk = [gkey] + [(gkey + 'g', c0) for c0 in allc
                                   if c0 < t0 + tn and c0 + (16 if c0 == 1024 else 128) > t0]
                    for m in range(8):
                        pb, pk = pbR.next()
                        for e4 in range(4):
                            mm(pb[:, 0:tn], wv[:, e4, m * 128:(m + 1) * 128], gbuf[:, e4, t0:t0 + tn],
                               e4 == 0, e4 == 3, [wk] + gk, [pk])
                        hadd(m, ti, t0, tn, pb, pk)
                        if tick is not None and ti < 2:
                            tick()

            if half == 0:
                tiles = [(8, 16, 1024)] + [(t, 128, t * 128) for t in range(8)]
                for h in range(4):
                    proj(h, 0, lambda: None)
                    chunks(h, 0, tiles)
                    outproj(h, 0, tiles)
            else:
                tiles = [(t, 128, t * 128) for t in range(8)]
                def make_tick(hp, nsteps):
                    cnt = [0, 0]

                    def tick():
                        cnt[0] += 1
                        while cnt[1] < 16 and cnt[1] * nsteps < cnt[0] * 16:
                            sample_unit(hp, cnt[1])
                            cnt[1] += 1
                    return tick, cnt
                for h in range(4):
                    if h == 0:
                        proj(h, 0, lambda: None)
                    else:
                        tick, cnt = make_tick(h - 1, 27 + 16)
                        proj(h, h % 2, tick)

                        def fin(h=h, cnt=cnt):
                            assert cnt[1] == 16, cnt
                            sample_finish(h - 1, (h - 1) % 2)
                        outproj(h - 1, (h - 1) % 2, tiles, tick=tick, before_special=fin)
                    chunks(h, h % 2, tiles)
                    sample_prep(h)
                tick, cnt = make_tick(3, 16)

                def fin3(cnt=cnt):
                    assert cnt[1] == 16, cnt
                    sample_finish(3, 1)
                outproj(3, 1, tiles, tick=tick, before_special=fin3)

        for half in range(2):
            c0 = half * NT
            if half == 1:
                P.dma('sp', [(RC_prev[:, 0, :], sp_prev[0:128, :]), (RC_prev[0:112, 1, :], sp_prev[128:240, :])],
                      writes=['RC'])
            xv = xT.rearrange("(k p) n -> p k n", p=128)
            for c in range(8):
                P.dma('sp', [(hT[:, c, :], xv[:, c, c0:c0 + NT])], writes=hk(c), semkey=('hload', c))
            P.dma('sp', [(cs[:, 0, :], csd[0, :, c0:c0 + NT]), (cs[:, 1, :], csd[1, :, c0:c0 + NT])], writes=['cs'])
            pool_layer(half)
            dump(0, half)
            ffn(0)
            dump(1, half)
            ret_layer(half)
            dump(2, half)
            ffn(1)
            dump(3, half)
            norm(C_NFIN, 'final')
            yv = yT.rearrange("(k p) n -> p k n", p=128)
            for c in range(8):
                P.dma('sp', [(yv[:, c, c0:c0 + NT], hT[:, c, :])], reads=hk(c), semkey=('o_y', half, c), is_out=True)
        assert wstate['cur'] == len(wl), (wstate, len(wl))
        P.finish()
        stats = P.emit()
    return nc, stats


def _const_tables():
    lg = _gammas()
    cfc = np.zeros((128, NCF), np.float32)
    idx = np.arange(128, dtype=np.float32)
    for h in range(4):
        cfc[:, C_KDEC + h] = np.exp((np.float32(127.0) - idx) * lg[h])
        cfc[:16, C_KDECM + h] = np.exp((np.float32(15.0) - idx[:16]) * lg[h])
        rel = idx[None, :] - idx[:, None]
        m = np.where(rel >= 0, np.exp(np.maximum(rel, 0) * lg[h]), 0.0).astype(np.float32) * np.float32(1.0 / 16.0)
        cfc[:, C_MASK + h * 128:C_MASK + (h + 1) * 128] = m
        cfc[:, C_CROSS + h * 128:C_CROSS + (h + 1) * 128] = (np.exp((idx + 1.0) * lg[h]) * np.float32(1.0 / 16.0))[None, :]
    for g, w in enumerate(WINS):
        t = np.arange(16)
        cfc[:, C_INVC + g * 16:C_INVC + (g + 1) * 16] = (1.0 / np.minimum(t + 1, w)).astype(np.float32)[None, :]
    cfc[:16, C_I16:C_I16 + 16] = np.eye(16, dtype=np.float32)
    cfc[:, C_D16:C_D16 + 256] = (np.eye(16, dtype=np.float32) / 16.0).reshape(1, 256)
    for kc in range(2):
        for g, w in enumerate(WINS):
            for r in range(128):
                R = kc * 128 + r
                if R >= 240:
                    continue
                b, j = divmod(R, 15)
                if j >= 16 - w:
                    cfc[r, C_AW + (kc * 4 + g) * 16 + b] = 1.0
    cfc[:, C_ID:C_ID + 128] = np.eye(128, dtype=np.float32)
    theta = (1.0 / (np.float32(10000.0) ** np.linspace(0.0, 1.0, 128, dtype=np.float32))).astype(np.float32)
    pos = np.concatenate([np.arange(16, 1040), np.arange(0, 16), np.arange(1040, 2064), np.full(16, 16384)]).astype(np.float32)
    ang = (pos[None, :] * theta[:, None]).astype(np.float32)
    cs = np.stack([np.cos(ang), np.sin(ang)]).astype(np.float32)
    return cfc, cs


def _vec8(v):
    return np.ascontiguousarray(v.reshape(-1, 128).T)


_CACHE = {}


def kernel(x_prompt, x_sample, state_pool, state_ret, meta_tokens, norm_mix, norm_ffn, norm_final,
           w_pool, pool_scale, w_ret_in, ret_gn_w, ret_gn_b, w_ret_out, w_ffn_gate, w_ffn_up, w_ffn_down, _dbg=False):
    f = lambda a: np.asarray(a, dtype=np.float32)
    x_prompt, x_sample, state_pool, state_ret, meta_tokens = map(f, (x_prompt, x_sample, state_pool, state_ret, meta_tokens))
    norm_mix, norm_ffn, norm_final, w_pool, pool_scale = map(f, (norm_mix, norm_ffn, norm_final, w_pool, pool_scale))
    w_ret_in, ret_gn_w, ret_gn_b, w_ret_out = map(f, (w_ret_in, ret_gn_w, ret_gn_b, w_ret_out))
    w_ffn_gate, w_ffn_up, w_ffn_down = map(f, (w_ffn_gate, w_ffn_up, w_ffn_down))

    key = bool(_dbg)
    if key not in _CACHE:
        _CACHE[key] = build_program(dbg=_dbg)
    nc, _ = _CACHE[key]

    cfc, cs = _const_tables()
    cfc[:, C_NM0:C_NM0 + 8] = _vec8(norm_mix[0])
    cfc[:, C_NM1:C_NM1 + 8] = _vec8(norm_mix[1])
    cfc[:, C_NF0:C_NF0 + 8] = _vec8(norm_ffn[0])
    cfc[:, C_NF1:C_NF1 + 8] = _vec8(norm_ffn[1])
    cfc[:, C_NFIN:C_NFIN + 8] = _vec8(norm_final)
    cfc[:, C_PS:C_PS + 8] = _vec8(pool_scale[0])
    cfc[:, C_GNW:C_GNW + 16] = _vec8(ret_gn_w[0])
    cfc[:, C_GNB:C_GNB + 16] = _vec8(ret_gn_b[0])

    wpool_h = np.ascontiguousarray(w_pool[0].reshape(4, 2, 128, 256).transpose(2, 0, 1, 3).reshape(128, 2048))
    w_gu = np.empty((2, 11, 1024, 512), np.float32)
    for l in range(2):
        w_gu[l, :, :, 0:256] = w_ffn_gate[l].reshape(1024, 11, 256).transpose(1, 0, 2)
        w_gu[l, :, :, 256:512] = w_ffn_up[l].reshape(1024, 11, 256).transpose(1, 0, 2)
    w_d0 = np.ascontiguousarray(w_ffn_down[:, 0:1536, :].reshape(2, 1536, 4, 256).transpose(0, 2, 1, 3))
    w_d1 = np.ascontiguousarray(w_ffn_down[:, 1536:2816, :].reshape(2, 1280, 4, 256).transpose(0, 2, 1, 3))
    wi = w_ret_in[0]
    w_in = np.empty((4, 3, 1024, 512), np.float32)
    for h in range(4):
        q = wi[:, h * 256:(h + 1) * 256]
        k = wi[:, 1024 + h * 256:1024 + (h + 1) * 256]
        w_in[h, 0, :, 0:128] = q[:, 0::2]
        w_in[h, 0, :, 128:256] = q[:, 1::2]
        w_in[h, 0, :, 256:384] = k[:, 0::2]
        w_in[h, 0, :, 384:512] = k[:, 1::2]
        w_in[h, 1] = wi[:, 2048 + h * 512:2048 + (h + 1) * 512]
        w_in[h, 2] = wi[:, 4096 + h * 512:4096 + (h + 1) * 512]
    w_out = np.ascontiguousarray(w_ret_out[0].reshape(4, 512, 1024))

    def _pm(a, K):
        lead, N = a.shape[:-2], a.shape[-1]
        return np.ascontiguousarray(a.reshape(*lead, K, 128, N).swapaxes(-3, -2).reshape(*lead, 128, K * N))
    w_gu_pm, w_d0_pm, w_d1_pm = _pm(w_gu, 8), _pm(w_d0, 12), _pm(w_d1, 10)
    w_in_pm, w_out_pm = _pm(w_in, 8), _pm(w_out, 4)

    in_maps = []
    for b in range(NCORES):
        xs_ = x_sample[b * 16:(b + 1) * 16, 0, :]
        cols = np.concatenate([x_prompt[b, 0:1024], meta_tokens, x_prompt[b, 1024:2048], xs_], axis=0)
        sr = state_ret[0, b * 16:(b + 1) * 16].reshape(16, 4, 128, 1024)
        in_maps.append({
            "xT": np.ascontiguousarray(cols.T),
            "cs": cs, "cf": cfc,
            "sp_prev": np.ascontiguousarray(state_pool[0, b * 16:(b + 1) * 16].reshape(240, 1024)),
            "sret": np.ascontiguousarray(sr),
            "wpool": wpool_h, "w_gu": w_gu_pm, "w_d0": w_d0_pm, "w_d1": w_d1_pm, "w_in": w_in_pm, "w_out": w_out_pm,
        })
    res = run_bass_kernel_spmd(nc, in_maps, core_ids=list(range(NCORES)))
    R = res.results

    y_prompt = np.empty((8, 2048, 1024), np.float32)
    y_sample = np.empty((128, 1, 1024), np.float32)
    npp_o = np.empty((1, 8, 15, 1024), np.float32)
    nps_o = np.empty((1, 128, 15, 1024), np.float32)
    nrp_o = np.empty((1, 8, 4, 256, 512), np.float32)
    nrs_o = np.empty((1, 128, 4, 256, 512), np.float32)
    for b in range(NCORES):
        r = R[b]
        yt = r["yT"]
        y_prompt[b, 0:1024] = yt[:, 0:1024].T
        y_prompt[b, 1024:2048] = yt[:, NT:NT + 1024].T
        y_sample[b * 16:(b + 1) * 16, 0, :] = yt[:, NT + 1024:NT + 1040].T
        npp_o[0, b] = r["npp"][:, 1:16].T
        nps_o[0, b * 16:(b + 1) * 16, 0:14] = r["nps_prev"]
        nps_o[0, b * 16:(b + 1) * 16, 14] = r["nps_x"].T
        nrp_o[0, b] = r["nrp"].transpose(0, 2, 1, 3).reshape(4, 256, 512)
        nrs_o[0, b * 16:(b + 1) * 16] = r["nrs"].reshape(16, 4, 256, 512)
    outs = (y_prompt, y_sample, npp_o, nps_o, nrp_o, nrs_o)
    if _dbg:
        return outs, R
    return outs
```
